# Optimizing a Trainium2 kernel written in Bass

```python
import math
import jax, jax.numpy as jnp
from jax import lax
import numpy as np

D_MODEL = 1024
BATCH = 8
SEQ = 2048
DEPTH = 2

HEAD_DIM = 64
SWA_Q_HEADS = 8
SWA_KV_HEADS = 2
WINDOW = 128
BLOCK = 128
DIFF_HEADS = 4
DIFF_QK_DIM = 64
DIFF_V_DIM = 2 * DIFF_QK_DIM
D_FF = ((8 * D_MODEL // 3 + 127) // 128) * 128
CONV_WIDTH = 3
ROPE_THETA = 10000.0
EPS = 1e-6
NEG = -1e30

SWA_Q = SWA_Q_HEADS * HEAD_DIM
SWA_KV = SWA_KV_HEADS * HEAD_DIM
DIFF_Q = DIFF_HEADS * 2 * DIFF_QK_DIM
DIFF_V = DIFF_HEADS * DIFF_V_DIM
IN_COLS = SWA_Q + 2 * SWA_KV + 2 * DIFF_Q + DIFF_V
MIX_WIDTH = SWA_Q + DIFF_V

kernel_name = "hybrid_swa_sink_diffattn_convglu"


def rmsnorm(x, g):
    xf = x.astype(jnp.float32)
    y = xf * lax.rsqrt(jnp.mean(xf * xf, axis=-1, keepdims=True) + EPS)
    return (y * g.astype(jnp.float32)).astype(x.dtype)


def rope_tables(seq, dim):
    inv = 1.0 / (ROPE_THETA ** (jnp.arange(0, dim, 2, dtype=jnp.float32) / dim))
    ang = jnp.arange(seq, dtype=jnp.float32)[:, None] * inv[None, :]
    return jnp.cos(ang), jnp.sin(ang)


def apply_rope(x, cos, sin):
    x1, x2 = jnp.split(x.astype(jnp.float32), 2, axis=-1)
    c = cos[None, :, None, :]
    s = sin[None, :, None, :]
    return jnp.concatenate([x1 * c - x2 * s, x2 * c + x1 * s], axis=-1).astype(x.dtype)


def windowed_gqa_sink(q, k, v, sink):
    B, S, Hq, D = q.shape
    Hkv = k.shape[2]
    G = Hq // Hkv
    nb = S // BLOCK
    qb = q.reshape(B, nb, BLOCK, Hkv, G, D)

    def band(t):
        tp = jnp.pad(t, ((0, 0), (BLOCK, BLOCK), (0, 0), (0, 0)))
        tb = tp.reshape(B, nb + 2, BLOCK, Hkv, D)
        return jnp.concatenate([tb[:, :-2], tb[:, 1:-1], tb[:, 2:]], axis=2)

    kb, vb = band(k), band(v)
    scores = jnp.einsum('bnqhgd,bnkhd->bnhgqk', qb, kb,
                        preferred_element_type=jnp.float32) * (D ** -0.5)
    qpos = jnp.arange(nb)[:, None] * BLOCK + jnp.arange(BLOCK)[None, :]
    kpos = (jnp.arange(nb)[:, None] - 1) * BLOCK + jnp.arange(3 * BLOCK)[None, :]
    rel = kpos[:, None, :] - qpos[:, :, None]
    valid = (jnp.abs(rel) <= WINDOW) & (kpos[:, None, :] >= 0) & (kpos[:, None, :] < S)
    scores = jnp.where(valid[None, :, None, None], scores, NEG)
    sink_l = sink.astype(jnp.float32).reshape(Hkv, G)[None, None, :, :, None, None]
    m = jnp.maximum(jnp.max(scores, axis=-1, keepdims=True), sink_l)
    p = jnp.exp(scores - m)
    p = p / (jnp.sum(p, axis=-1, keepdims=True) + jnp.exp(sink_l - m))
    out = jnp.einsum('bnhgqk,bnkhd->bnqhgd', p.astype(v.dtype), vb)
    return out.reshape(B, S, Hq, D)


def diff_attention(q, k, v, lam):
    B, S, H, _, Dk = q.shape
    nb = S // BLOCK
    qb = jnp.moveaxis(q.reshape(B, nb, BLOCK, H, 2, Dk), 1, 0)
    scale = Dk ** -0.5

    def one_block(qi):
        s = jnp.einsum('bqhcd,bkhcd->bhcqk', qi, k,
                       preferred_element_type=jnp.float32) * scale
        p = jax.nn.softmax(s, axis=-1)
        w = p[:, :, 0] - lam * p[:, :, 1]
        return jnp.einsum('bhqk,bkhd->bqhd', w.astype(v.dtype), v)

    out = lax.map(one_block, qb)
    return jnp.moveaxis(out, 0, 1).reshape(B, S, H, v.shape[-1])


def centred_dwconv(x, w, b):
    S = x.shape[1]
    half = CONV_WIDTH // 2
    xp = jnp.pad(x, ((0, 0), (half, half), (0, 0)))
    out = b
    for j in range(CONV_WIDTH):
        out = out + xp[:, j:j + S, :] * w[j]
    return out


def setup_inputs(seed: int = 0) -> dict:
    key = jax.random.key(seed)
    ks = jax.random.split(key, 20)
    f32 = jnp.float32
    nrm = lambda k, shp, sc: jax.random.normal(k, shp, f32) * sc
    L = DEPTH
    return {
        "x": nrm(ks[0], (BATCH, SEQ, D_MODEL), 1.0),
        "g_attn": 1.0 + nrm(ks[1], (L, D_MODEL), 0.02),
        "w_in": nrm(ks[2], (L, D_MODEL, IN_COLS), D_MODEL ** -0.5),
        "qn_a": 1.0 + nrm(ks[3], (L, HEAD_DIM), 0.02),
        "kn_a": 1.0 + nrm(ks[4], (L, HEAD_DIM), 0.02),
        "sink": nrm(ks[5], (L, SWA_Q_HEADS), 0.5),
        "qn_b": 1.0 + nrm(ks[6], (L, DIFF_QK_DIM), 0.02),
        "kn_b": 1.0 + nrm(ks[7], (L, DIFF_QK_DIM), 0.02),
        "lq1": nrm(ks[8], (L, DIFF_QK_DIM), 0.1),
        "lk1": nrm(ks[9], (L, DIFF_QK_DIM), 0.1),
        "lq2": nrm(ks[10], (L, DIFF_QK_DIM), 0.1),
        "lk2": nrm(ks[11], (L, DIFF_QK_DIM), 0.1),
        "subln": 1.0 + nrm(ks[12], (L, DIFF_V_DIM), 0.02),
        "w_out": nrm(ks[13], (L, MIX_WIDTH, D_MODEL), MIX_WIDTH ** -0.5),
        "g_ffn": 1.0 + nrm(ks[14], (L, D_MODEL), 0.02),
        "w_up": nrm(ks[15], (L, D_MODEL, 2 * D_FF), D_MODEL ** -0.5),
        "conv_w": nrm(ks[16], (L, CONV_WIDTH, D_FF), CONV_WIDTH ** -0.5),
        "conv_b": nrm(ks[17], (L, D_FF), 0.02),
        "w_down": nrm(ks[18], (L, D_FF, D_MODEL), D_FF ** -0.5),
    }


def reference(x, g_attn, w_in, qn_a, kn_a, sink, qn_b, kn_b, lq1, lk1, lq2, lk2,
              subln, w_out, g_ffn, w_up, conv_w, conv_b, w_down):
    B, S, _ = x.shape
    cos_a, sin_a = rope_tables(S, HEAD_DIM)
    cos_b, sin_b = rope_tables(S, DIFF_QK_DIM)
    offs = np.cumsum([SWA_Q, SWA_KV, SWA_KV, DIFF_Q, DIFF_Q]).tolist()
    for l in range(DEPTH):
        lambda_init = 0.8 - 0.6 * math.exp(-0.3 * l)
        h = rmsnorm(x, g_attn[l])
        proj = jnp.einsum('bsd,dc->bsc', h, w_in[l])
        qa, ka, va, qb, kb, vb = jnp.split(proj, offs, axis=-1)
        qa = apply_rope(rmsnorm(qa.reshape(B, S, SWA_Q_HEADS, HEAD_DIM), qn_a[l]), cos_a, sin_a)
        ka = apply_rope(rmsnorm(ka.reshape(B, S, SWA_KV_HEADS, HEAD_DIM), kn_a[l]), cos_a, sin_a)
        va = va.reshape(B, S, SWA_KV_HEADS, HEAD_DIM)
        ya = windowed_gqa_sink(qa, ka, va, sink[l])
        qb = apply_rope(rmsnorm(qb.reshape(B, S, 2 * DIFF_HEADS, DIFF_QK_DIM), qn_b[l]), cos_b, sin_b)
        kb = apply_rope(rmsnorm(kb.reshape(B, S, 2 * DIFF_HEADS, DIFF_QK_DIM), kn_b[l]), cos_b, sin_b)
        qb = qb.reshape(B, S, DIFF_HEADS, 2, DIFF_QK_DIM)
        kb = kb.reshape(B, S, DIFF_HEADS, 2, DIFF_QK_DIM)
        vb = vb.reshape(B, S, DIFF_HEADS, DIFF_V_DIM)
        lam = (jnp.exp(jnp.sum(lq1[l].astype(jnp.float32) * lk1[l].astype(jnp.float32)))
               - jnp.exp(jnp.sum(lq2[l].astype(jnp.float32) * lk2[l].astype(jnp.float32)))
               + lambda_init)
        yb = diff_attention(qb, kb, vb, lam)
        yb = rmsnorm(yb, subln[l]) * (1.0 - lambda_init)
        y = jnp.concatenate([ya.reshape(B, S, SWA_Q), yb.reshape(B, S, DIFF_V)], axis=-1)
        x = x + jnp.einsum('bsm,md->bsd', y, w_out[l])
        h = rmsnorm(x, g_ffn[l])
        gate, val = jnp.split(jnp.einsum('bsd,df->bsf', h, w_up[l]), 2, axis=-1)
        gate = centred_dwconv(gate, conv_w[l], conv_b[l])
        x = x + jnp.einsum('bsf,fd->bsd', jax.nn.silu(gate) * val, w_down[l])
    return x
```

```python
import math
import numpy as np
from contextlib import ExitStack
import concourse.bass as bass
import concourse.mybir as mybir
from concourse.bass_utils import run_bass_kernel_spmd

F32 = mybir.dt.float32
BF16 = mybir.dt.bfloat16
AF = mybir.ActivationFunctionType
ALU = mybir.AluOpType
AX = mybir.AxisListType

S = 2048
D = 1024
NB = 16
DFF = 2816
NJ = 22
L_TOTAL = 2
EPS = 1e-6
O_GA, O_GF, O_QKG, O_SINK, O_LAM, O_SUB, O_CW, O_CB = 0, 8, 16, 272, 280, 536, 537, 603
NS = 625
KV_COLS = 1408
Q_COLS = 1024
FF_GROUPS = [list(range(0, 6)), list(range(6, 11)), list(range(11, 17)), list(range(17, 22))]


class Buf:
    __slots__ = ("name", "w", "r")

    def __init__(self, name=""):
        self.name = name
        self.w = {}
        self.r = {}


class Slot:
    def __init__(self, sem):
        self.sem = sem
        self.count = 0


class Trk:
    def __init__(self, nc, es):
        self.nc = nc
        self.es = es
        self.engs = {"pe": nc.tensor, "act": nc.scalar, "dve": nc.vector, "pool": nc.gpsimd, "sp": nc.sync}
        self.sems = []
        self.cur = {}
        self.esem = {}
        for e in self.engs:
            self.esem[e] = self.new_sem("s_" + e)
        self.tick = {e: 0 for e in self.engs}
        self.known = {e: {} for e in self.engs}

    def new_sem(self, name):
        s = self.es.enter_context(self.nc.semaphore(name))
        self.sems.append(s)
        self.cur[len(self.sems) - 1] = 0
        return len(self.sems) - 1

    def new_slot(self, name):
        return Slot(self.new_sem(name))

    def wait(self, e, reads=(), writes=()):
        own = self.esem[e]
        need = {}
        for b in reads:
            for s, v in b.w.items():
                if need.get(s, 0) < v:
                    need[s] = v
        pe_own = own if e == "pe" else -1
        for b in writes:
            for s, v in b.w.items():
                if s != pe_own and need.get(s, 0) < v:
                    need[s] = v
            for s, v in b.r.items():
                if need.get(s, 0) < v:
                    need[s] = v
        k = self.known[e]
        for s, v in need.items():
            if k.get(s, 0) < v:
                self.engs[e].wait_ge(self.sems[s], v)
                k[s] = v

    def done(self, e, ins, reads=(), writes=()):
        self.tick[e] += 1
        own = self.esem[e]
        ins.then_inc(self.sems[own], 1)
        t = self.tick[e]
        self.cur[own] = t
        for b in writes:
            b.w[own] = t
        for b in reads:
            b.r[own] = t

    def op(self, e, fn, reads=(), writes=()):
        self.wait(e, reads, writes)
        ins = fn(self.engs[e])
        self.done(e, ins, reads, writes)
        return ins

    def dma(self, q, slot, out, in_, reads=(), writes=(), **kw):
        self.wait(q, reads, writes)
        ins = self.engs[q].dma_start(out=out, in_=in_, **kw)
        slot.count += 16
        ins.then_inc(self.sems[slot.sem], 16)
        self.cur[slot.sem] = slot.count
        for b in writes:
            b.w[slot.sem] = slot.count
        for b in reads:
            b.r[slot.sem] = slot.count

    def barrier(self, engines=("pe", "act", "dve", "pool", "sp")):
        for e in engines:
            k = self.known[e]
            for s, v in self.cur.items():
                if v > 0 and k.get(s, 0) < v:
                    self.engs[e].wait_ge(self.sems[s], v)
                    k[s] = v


def build(layers=(0, 1), first=True, last=True):
    nc = bass.Bass("TRN2", target_bir_lowering=False)

    def dram(name, shape, kind="ExternalInput"):
        return nc.dram_tensor(name, shape, F32, kind=kind).ap()

    x_d = dram("x", [S, D])
    y_d = dram("y", [S, D], kind="ExternalOutput")
    cos_d = dram("cosT", [128, NB * 32])
    sin_d = dram("sinS", [128, NB * 64])
    small_d = dram("small", [128, L_TOTAL * NS])
    wkv_d = dram("wkv", [L_TOTAL, 128, 8 * KV_COLS])
    wq_d = dram("wq", [L_TOTAL, 128, 8 * Q_COLS])
    wo_d = dram("wo", [L_TOTAL, 128, 8 * D])
    wup_d = dram("wup", [L_TOTAL, NJ, 128, 8 * 256])
    wdn_d = dram("wdn", [L_TOTAL, 128, NJ * D])

    with ExitStack() as es:
        T = Trk(nc, es)

        _cnt = [0]

        def sb(stack, name, shape, dt):
            _cnt[0] += 1
            return stack.enter_context(nc.sbuf_tensor(f"{name}_{_cnt[0]}", shape, dt))

        def act(fn, reads=(), writes=()):
            return T.op("act", fn, reads, writes)

        def dve(fn, reads=(), writes=()):
            return T.op("dve", fn, reads, writes)

        def pool(fn, reads=(), writes=()):
            return T.op("pool", fn, reads, writes)

        def pe_group(emit, reads=(), writes=()):
            T.wait("pe", reads, writes)
            lastins = emit(nc.tensor)
            T.done("pe", lastins, reads, writes)

        X = sb(es, "X", [128, NB * D], F32)
        Xv = X[:].rearrange("p (b d) -> p b d", b=NB)
        XB = [Buf(f"X{i}") for i in range(NB)]
        ident = sb(es, "ident", [128, 128], BF16)
        ones = sb(es, "ones", [128, 128], BF16)
        mge = sb(es, "mge", [128, 128], BF16)
        mle = sb(es, "mle", [128, 128], BF16)
        cosT = sb(es, "cosT_s", [128, NB * 32], F32)
        sinS = sb(es, "sinS_s", [128, NB * 64], F32)
        cosv = cosT[:].rearrange("p (b i) -> p b i", b=NB)
        sinv = sinS[:].rearrange("p (b t i) -> p b t i", b=NB, t=2)
        small = sb(es, "small_s", [128, L_TOTAL * NS], F32)
        misc = sb(es, "misc", [128, 64], F32)
        sel = sb(es, "sel", [64, 256], F32)
        constB = Buf("const")
        miscB = Buf("misc")
        rstd = sb(es, "rstd", [128, 16], F32)
        ssq = sb(es, "ssq", [128, 16], F32)
        sd16 = sb(es, "sd16", [128, 16], F32)
        rstdBs = [Buf(f"rstd{i}") for i in range(NB)]
        ssqBs = [Buf(f"ssq{i}") for i in range(NB)]
        eps_ap = misc[:, 0:1]
        neglam = misc[:, 1:2]
        gsub = misc[:, 2:3]
        esink = misc[:, 3:11]

        PSbig = es.enter_context(nc.psum_tensor("psbig", [128, 4096], F32))
        PS = [PSbig[:, i * 512:(i + 1) * 512] for i in range(8)]
        PB = [Buf(f"ps{i}") for i in range(8)]

        def ps_bf(i, k):
            return PS[i][:].bitcast(BF16).rearrange("p (k c) -> p k c", c=128)[:, 0:k, :]

        sX = [T.new_slot(f"sx{i}") for i in range(NB)]
        sW = [T.new_slot(f"sw{i}") for i in range(8)]
        sU = [T.new_slot(f"su{i}") for i in range(3)]
        sD = [T.new_slot(f"sd{i}") for i in range(2)]
        sP = T.new_slot("sp0")

        xv_d = x_d.rearrange("(b p) d -> p b d", p=128)
        yv_d = y_d.rearrange("(b p) d -> p b d", p=128)
        T.dma("sp", sP, out=small[:], in_=small_d[:, :], writes=[constB])
        T.dma("sp", sP, out=cosT[:], in_=cos_d[:, :], writes=[constB])
        T.dma("sp", sP, out=sinS[:], in_=sin_d[:, :], writes=[constB])
        with ExitStack() as s0:
            tmpf = sb(s0, "tmpf", [128, 128], F32)
            tmpB = Buf("tmpf")
            pool(lambda e: e.memset(tmpf[:], 0.0), writes=[tmpB])
            pool(lambda e: e.affine_select(out=tmpf[:], in_=tmpf[:], pattern=[[-1, 128]], compare_op=ALU.not_equal,
                                           fill=1.0, base=0, channel_multiplier=1), reads=[tmpB], writes=[tmpB])
            dve(lambda e: e.tensor_copy(out=ident[:], in_=tmpf[:]), reads=[tmpB], writes=[constB])
            pool(lambda e: e.memset(tmpf[:], 1.0), reads=[tmpB], writes=[tmpB])
            dve(lambda e: e.tensor_copy(out=ones[:], in_=tmpf[:]), reads=[tmpB], writes=[constB])
            pool(lambda e: e.affine_select(out=tmpf[:], in_=tmpf[:], pattern=[[-1, 128]], compare_op=ALU.is_ge,
                                           fill=0.0, base=0, channel_multiplier=1), reads=[tmpB], writes=[tmpB])
            dve(lambda e: e.tensor_copy(out=mge[:], in_=tmpf[:]), reads=[tmpB], writes=[constB])
            pool(lambda e: e.memset(tmpf[:], 1.0), reads=[tmpB], writes=[tmpB])
            pool(lambda e: e.affine_select(out=tmpf[:], in_=tmpf[:], pattern=[[1, 128]], compare_op=ALU.is_ge,
                                           fill=0.0, base=0, channel_multiplier=-1), reads=[tmpB], writes=[tmpB])
            dve(lambda e: e.tensor_copy(out=mle[:], in_=tmpf[:]), reads=[tmpB], writes=[constB])
            pool(lambda e: e.memset(sel[:], 0.0), writes=[constB])
            pool(lambda e: e.memset(sel[0:1, 0:128], 1.0), reads=[constB], writes=[constB])
            pool(lambda e: e.memset(sel[32:33, 128:256], 1.0), reads=[constB], writes=[constB])
            pool(lambda e: e.memset(misc[:], 0.0), writes=[miscB])
            pool(lambda e: e.memset(misc[:, 0:1], EPS), reads=[miscB], writes=[miscB])
            T.barrier()

        def sm(l, off, n):
            return small[:, l * NS + off: l * NS + off + n]

        def stat_of(tb, junk, junkB):
            act(lambda e: e.activation(out=junk, in_=Xv[:, tb, :], func=AF.Square, accum_out=ssq[:, tb:tb + 1]),
                reads=[XB[tb]], writes=[junkB, ssqBs[tb]])
            act(lambda e: e.activation(out=sd16[:, tb:tb + 1], in_=ssq[:, tb:tb + 1], func=AF.Sqrt, bias=eps_ap,
                                       scale=1.0 / D), reads=[ssqBs[tb], miscB], writes=[ssqBs[tb]])
            dve(lambda e: e.reciprocal(out=rstd[:, tb:tb + 1], in_=sd16[:, tb:tb + 1]), reads=[ssqBs[tb]],
                writes=[rstdBs[tb]])

        def make_hT(tb, g_ap, hb, hbB, hT_out, hTB, bank):
            act(lambda e: e.activation(out=hb[:], in_=Xv[:, tb, :], func=AF.Copy, scale=rstd[:, tb:tb + 1]),
                reads=[XB[tb], rstdBs[tb]], writes=[hbB])
            tp = ps_bf(bank, 8)

            def emit(pe):
                ins = None
                for k in range(8):
                    ins = pe.transpose(out=tp[:, k, :], in_=hb[:, k * 128:(k + 1) * 128], identity=ident[:])
                return ins
            pe_group(emit, reads=[hbB, constB], writes=[PB[bank]])
            dve(lambda e: e.tensor_tensor(out=hT_out, in0=tp, in1=g_ap.unsqueeze(2).to_broadcast([128, 8, 128]),
                                          op=ALU.mult), reads=[PB[bank], constB], writes=[hTB])

        for l in layers:
            lam_init = 0.8 - 0.6 * math.exp(-0.3 * l)
            def small_params(sa_stack, l=l, lam_init=lam_init):
                if True:
                    j256 = sb(sa_stack, "j256", [128, 64], F32)
                    jB = Buf("j256")
                    lamv = sm(l, O_LAM, 256).rearrange("p (t d) -> p t d", t=4)
                    dve(lambda e: e.scalar_tensor_tensor(out=j256[:], in0=lamv[:, 0, :], scalar=1.0, in1=lamv[:, 1, :],
                                                         op0=ALU.mult, op1=ALU.mult, accum_out=misc[:, 11:12]),
                        reads=[constB, miscB], writes=[jB, miscB])
                    dve(lambda e: e.scalar_tensor_tensor(out=j256[:], in0=lamv[:, 2, :], scalar=1.0, in1=lamv[:, 3, :],
                                                         op0=ALU.mult, op1=ALU.mult, accum_out=misc[:, 12:13]),
                        reads=[constB, miscB], writes=[jB, miscB])
                    act(lambda e: e.activation(out=misc[:, 13:15], in_=misc[:, 11:13], func=AF.Exp),
                        reads=[miscB], writes=[miscB])
                    act(lambda e: e.activation(out=esink, in_=sm(l, O_SINK, 8), func=AF.Exp),
                        reads=[constB, miscB], writes=[miscB])
                    dve(lambda e: e.tensor_tensor(out=neglam, in0=misc[:, 14:15], in1=misc[:, 13:14], op=ALU.subtract),
                        reads=[miscB], writes=[miscB])
                    dve(lambda e: e.tensor_scalar(out=neglam, in0=neglam, scalar1=-lam_init, scalar2=None, op0=ALU.add),
                        reads=[miscB], writes=[miscB])
                    dve(lambda e: e.tensor_scalar(out=gsub, in0=sm(l, O_SUB, 1), scalar1=1.0 - lam_init, scalar2=None,
                                                  op0=ALU.mult), reads=[constB, miscB], writes=[miscB])

            qkg = sm(l, O_QKG, 256).rearrange("p (t d) -> p t d", t=4)
            gA = sm(l, O_GA, 8)
            gF = sm(l, O_GF, 8)

            with ExitStack() as sa:
                KT = sb(sa, "KT", [128, 6 * S], BF16)
                KTv = KT[:].rearrange("p (k t) -> p k t", k=6)
                KTB = Buf("KT")
                vaug = sb(sa, "vaug", [128, NB * 2 * 66], BF16)
                vaugv = vaug[:].rearrange("p (b g d) -> p b g d", b=NB, g=2)
                vb = sb(sa, "vb", [128, NB * 512], BF16)
                vbv = vb[:].rearrange("p (b d) -> p b d", b=NB)
                VB = Buf("V")
                scrA = sb(sa, "scrA", [128, 3072], F32)
                arena2 = sb(sa, "arena2", [128, 3072], F32)
                scrs = [scrA, arena2]
                a2bf = arena2[:].bitcast(BF16)
                hb = [sb(sa, f"hb{i}", [128, D], BF16) for i in range(2)]
                hbB = [Buf("hb0"), Buf("hb1")]
                hT = [sb(sa, f"hT{i}", [128, 8 * 128], BF16) for i in range(2)]
                hTv = [t[:].rearrange("p (k c) -> p k c", k=8) for t in hT]
                hTB = [Buf("hT0"), Buf("hT1")]
                qtoks = [sb(sa, f"qtok{i}", [128, 1024], BF16) for i in range(2)]
                qtokBs = [Buf("qtok0"), Buf("qtok1")]
                hstats = [sb(sa, f"hstat{i}", [128, 64], F32) for i in range(2)]
                hstatBs = [Buf("hstat0"), Buf("hstat1")]
                pqBs, sqABs, BtBs = [Buf("pq0"), Buf("pq1")], [Buf("sqA0"), Buf("sqA1")], [Buf("Bt0"), Buf("Bt1")]

                pool(lambda e: e.memset(vaug[:], 1.0), writes=[VB])
                wq = sb(sa, "wq_s", [128, 8 * Q_COLS], BF16)
                wq_v = wq[:].rearrange("p (k c) -> p k c", k=8)
                wqB = Buf("wq")

                def run_pipeline(tbs, w_v, wB, bank_sets, tph_banks, tpq_banks, nqk, runs, dest_fn, v_fn, tq_off=0, pre_fn=None, ahead=False):
                    H = nqk // 64
                    nch = nqk // 128

                    def bufs(i):
                        par = i % 2
                        scr = scrs[par]
                        return dict(par=par, scr=scr, hstat=hstats[par], qtok=qtoks[par], pqB=pqBs[par], sqAB=sqABs[par],
                                    BtB=BtBs[par], qtokB=qtokBs[par], hstatB=hstatBs[par], pq=scr[:, 0:nqk],
                                    sqA=scr[:, 1024:1024 + nqk], Bt=scr[:, 2048:2048 + nqk])

                    def w_hb(i, tb):
                        par = i % 2
                        act(lambda e: e.activation(out=hb[par][:], in_=Xv[:, tb, :], func=AF.Copy, scale=rstd[:, tb:tb + 1]),
                            reads=[XB[tb], rstdBs[tb]], writes=[hbB[par]])

                    def w_th(i, tb):
                        par = i % 2
                        tp = ps_bf(tph_banks[par], 8)

                        def emit(pe):
                            ins = None
                            for k in range(8):
                                ins = pe.transpose(out=tp[:, k, :], in_=hb[par][:, k * 128:(k + 1) * 128], identity=ident[:])
                            return ins
                        pe_group(emit, reads=[hbB[par], constB], writes=[PB[tph_banks[par]]])

                    def w_hT(i, tb):
                        par = i % 2
                        tp = ps_bf(tph_banks[par], 8)
                        dve(lambda e: e.tensor_tensor(out=hTv[par], in0=tp, in1=gA.unsqueeze(2).to_broadcast([128, 8, 128]),
                                                      op=ALU.mult), reads=[PB[tph_banks[par]], constB], writes=[hTB[par]])

                    def w_proj(i, tb):
                        par = i % 2

                        def emit(pe):
                            ins = None
                            for k in range(8):
                                for (bk, c0, cn) in bank_sets[par]:
                                    ins = pe.matmul(PS[bk][:, 0:cn], lhsT=hTv[par][:, k, :], rhs=w_v[:, k, c0:c0 + cn],
                                                    start=(k == 0), stop=(k == 7))
                            return ins
                        pe_group(emit, reads=[hTB[par], wB], writes=[PB[bk] for (bk, _, _) in bank_sets[par]])

                    def w_evac(i, tb):
                        d = bufs(i)
                        for (bk, c0, cn) in bank_sets[d["par"]]:
                            if c0 >= nqk:
                                continue
                            n = min(cn, nqk - c0)
                            act(lambda e, bk=bk, c0=c0, n=n: e.activation(out=d["scr"][:, c0:c0 + n], in_=PS[bk][:, 0:n],
                                                                          func=AF.Copy),
                                reads=[PB[bk]], writes=[d["pqB"]])
                        if v_fn is not None:
                            v_fn(tb, bank_sets[d["par"]])
                        act(lambda e: e.activation(out=d["sqA"], in_=d["pq"], func=AF.Square), reads=[d["pqB"]],
                            writes=[d["sqAB"]])

                    def w_reduce(i, tb):
                        d = bufs(i)
                        dve(lambda e: e.tensor_reduce(out=d["hstat"][:, 0:H], in_=d["sqA"].rearrange("p (h d) -> p h d", d=64),
                                                      axis=AX.X, op=ALU.add), reads=[d["sqAB"]], writes=[d["hstatB"]])
                        pq3 = d["pq"].rearrange("p (h d) -> p h d", d=64)
                        for (h0, nh, gt) in runs:
                            pool(lambda e, h0=h0, nh=nh, gt=gt: e.tensor_tensor(
                                out=pq3[:, h0:h0 + nh, :], in0=pq3[:, h0:h0 + nh, :],
                                in1=qkg[:, gt, :].unsqueeze(1).to_broadcast([128, nh, 64]), op=ALU.mult),
                                reads=[d["pqB"], constB], writes=[d["pqB"]])

                    def w_sqrt(i, tb):
                        d = bufs(i)
                        hstat = d["hstat"]
                        act(lambda e: e.activation(out=hstat[:, 16:16 + H], in_=hstat[:, 0:H], func=AF.Sqrt, bias=eps_ap,
                                                   scale=1.0 / 64), reads=[d["hstatB"], miscB], writes=[d["hstatB"]])

                    def w_xn(i, tb):
                        d = bufs(i)
                        hstat = d["hstat"]
                        dve(lambda e: e.reciprocal(out=hstat[:, 32:32 + H], in_=hstat[:, 16:16 + H]),
                            reads=[d["hstatB"]], writes=[d["hstatB"]])
                        pq3 = d["pq"].rearrange("p (h d) -> p h d", d=64)
                        dve(lambda e: e.tensor_tensor(out=pq3, in0=pq3,
                                                      in1=hstat[:, 32:32 + H].unsqueeze(2).to_broadcast([128, H, 64]),
                                                      op=ALU.mult), reads=[d["pqB"], d["hstatB"]], writes=[d["pqB"]])

                    def w_AB(i, tb):
                        d = bufs(i)
                        pq4 = d["pq"].rearrange("p (h t i) -> p h t i", t=2, i=32)
                        A4 = d["sqA"].rearrange("p (h t i) -> p h t i", t=2, i=32)
                        B4 = d["Bt"].rearrange("p (h t i) -> p h t i", t=2, i=32)
                        Hh = H // 2
                        dve(lambda e: e.tensor_tensor(out=A4[:, 0:Hh], in0=pq4[:, 0:Hh],
                                                      in1=cosv[:, tb, :].unsqueeze(1).unsqueeze(1).to_broadcast([128, Hh, 2, 32]),
                                                      op=ALU.mult), reads=[d["pqB"], constB], writes=[d["sqAB"]])
                        pool(lambda e: e.tensor_tensor(out=A4[:, Hh:H], in0=pq4[:, Hh:H],
                                                       in1=cosv[:, tb, :].unsqueeze(1).unsqueeze(1).to_broadcast([128, H - Hh, 2, 32]),
                                                       op=ALU.mult), reads=[d["pqB"], constB], writes=[d["sqAB"]])
                        pool(lambda e: e.tensor_tensor(out=B4, in0=pq4,
                                                       in1=sinv[:, tb, :, :].unsqueeze(1).to_broadcast([128, H, 2, 32]),
                                                       op=ALU.mult), reads=[d["pqB"], constB], writes=[d["BtB"]])

                    def w_o(i, tb):
                        d = bufs(i)
                        A4 = d["sqA"].rearrange("p (h t i) -> p h t i", t=2, i=32)
                        B4 = d["Bt"].rearrange("p (h t i) -> p h t i", t=2, i=32)
                        o4 = d["qtok"][:, 0:nqk].rearrange("p (h t i) -> p h t i", t=2, i=32)
                        dve(lambda e: e.tensor_tensor(out=o4[:, :, 0, :], in0=A4[:, :, 0, :], in1=B4[:, :, 1, :], op=ALU.add),
                            reads=[d["sqAB"], d["BtB"]], writes=[d["qtokB"]])
                        dve(lambda e: e.tensor_tensor(out=o4[:, :, 1, :], in0=A4[:, :, 1, :], in1=B4[:, :, 0, :], op=ALU.add),
                            reads=[d["sqAB"], d["BtB"]], writes=[d["qtokB"]])

                    def w_tq(i, tb):
                        d = bufs(i)
                        tpb = tpq_banks[d["par"]]
                        tp = ps_bf(tpb, nch)

                        def emit_t(pe):
                            ins = None
                            for k in range(nch):
                                ins = pe.transpose(out=tp[:, k, :], in_=d["qtok"][:, k * 128:(k + 1) * 128], identity=ident[:])
                            return ins
                        pe_group(emit_t, reads=[d["qtokB"], constB], writes=[PB[tpb]])

                    def w_dest(i, tb):
                        tpb = tpq_banks[i % 2]
                        dest_fn(ps_bf(tpb, nch), nch, tb, tpb)

                    n = len(tbs)

                    def at(fn, idx):
                        if 0 <= idx < n:
                            fn(idx, tbs[idx])
                    ah = 1 if ahead else 0
                    if ahead:
                        at(w_hb, 0)
                        at(w_th, 0)
                        at(w_hT, 0)
                    for step in range(n + 2 + tq_off):
                        at(w_hb, step + ah)
                        if pre_fn is not None and step < n:
                            pre_fn(tbs[step])
                        at(w_evac, step - 1)
                        at(w_AB, step - 2)
                        at(w_th, step + ah)
                        if tq_off:
                            at(w_tq, step - 2 - tq_off)
                            at(w_dest, step - 2 - tq_off)
                        at(w_hT, step + ah)
                        at(w_proj, step)
                        at(w_reduce, step - 1)
                        at(w_sqrt, step - 1)
                        at(w_o, step - 2)
                        if not tq_off:
                            at(w_tq, step - 2)
                        at(w_xn, step - 1)
                        if not tq_off:
                            at(w_dest, step - 2)

                with ExitStack() as s1:
                    wkv = sb(s1, "wkv_s", [128, 8 * KV_COLS], BF16)
                    wkv_v = wkv[:].rearrange("p (k c) -> p k c", k=8)
                    wB = Buf("wkv")
                    wsrc = wkv_d[l].rearrange("p (k c) -> p k c", k=8)
                    for i in range(4):
                        T.dma("pool", sW[i], out=wkv_v[:, 2 * i:2 * i + 2, :], in_=wsrc[:, 2 * i:2 * i + 2, :],
                              writes=[wB], max_dma_last_dim=4096)
                    small_params(s1)
                    qsrc = wq_d[l].rearrange("p (k c) -> p k c", k=8)

                    def issue_wq(i):
                        T.dma("pool", sW[4 + i], out=wq_v[:, 2 * i:2 * i + 2, :], in_=qsrc[:, 2 * i:2 * i + 2, :],
                              writes=[wqB], max_dma_last_dim=4096)
                    WQ_STEPS = {5: 0, 7: 1, 9: 2, 11: 3}

                    def pre_fn(tb):
                        if tb in WQ_STEPS:
                            issue_wq(WQ_STEPS[tb])
                    if l == layers[0]:
                        for tb in range(NB):
                            T.dma("sp", sX[tb], out=Xv[:, tb, :], in_=xv_d[:, tb, :], writes=[XB[tb]])
                        junk1 = sb(s1, "junk1", [128, D], BF16)
                        junk1B = Buf("junk1")
                        for tb in range(2):
                            stat_of(tb, junk1[:], junk1B)

                        def pre_fn(tb):
                            if tb in WQ_STEPS:
                                issue_wq(WQ_STEPS[tb])
                            if tb + 2 < NB:
                                stat_of(tb + 2, junk1[:], junk1B)

                    def v_fn(tb, bset):
                        b1, b2 = bset[1][0], bset[2][0]
                        act(lambda e: e.activation(out=vaugv[:, tb, :, 0:64],
                                                   in_=PS[b1][:, 256:384].rearrange("p (g d) -> p g d", g=2),
                                                   func=AF.Copy), reads=[PB[b1]], writes=[VB])
                        act(lambda e: e.activation(out=vbv[:, tb, :], in_=PS[b2][:, :], func=AF.Copy),
                            reads=[PB[b2]], writes=[VB])

                    def dest_k(tp, nch, tb, tpb):
                        act(lambda e: e.activation(out=KTv[:, :, tb * 128:(tb + 1) * 128], in_=tp, func=AF.Copy),
                            reads=[PB[tpb]], writes=[KTB])

                    run_pipeline(list(range(NB)), wkv_v, wB,
                                 [[(0, 0, 512), (1, 512, 384), (2, 896, 512)], [(3, 0, 512), (4, 512, 384), (5, 896, 512)]],
                                 [7, 7], [6, 6], 768, [(0, 8, 3), (8, 4, 1)], dest_k, v_fn, tq_off=1, pre_fn=pre_fn, ahead=True)
                    T.barrier()

                with ExitStack() as s2:
                    wo = sb(s2, "wo_s", [128, 8 * D], BF16)
                    wo_v = wo[:].rearrange("p (k c) -> p k c", k=8)
                    wB = wqB
                    woB = Buf("wo")
                    osrc = wo_d[l].rearrange("p (k c) -> p k c", k=8)
                    for i in range(4):
                        T.dma("pool", sW[i], out=wo_v[:, 2 * i:2 * i + 2, :], in_=osrc[:, 2 * i:2 * i + 2, :],
                              writes=[woB], max_dma_last_dim=4096)
                    QT = sb(s2, "QT", [128, 8 * 512], BF16)
                    QTv = QT[:].rearrange("p (k t) -> p k t", k=8)
                    QTB = Buf("QT")
                    yT = sb(s2, "yT", [128, 8 * 512], BF16)
                    yTv = yT[:].rearrange("p (k t) -> p k t", k=8)
                    yTB = Buf("yT")
                    yav = a2bf[:, 0:2048].rearrange("p (n c) -> p n c", n=4)
                    yaB = Buf("ya")
                    Ew = [a2bf[:, 2048 + i * 512:2048 + (i + 1) * 512] for i in range(6)]
                    EwB = [Buf(f"Ew{i}") for i in range(6)]
                    Ed = [a2bf[:, 2048 + i * 1024:2048 + (i + 1) * 1024] for i in range(2)]
                    EdB = [Buf(f"Ed{i}") for i in range(2)]
                    sqd = a2bf[:, 4096:4608]
                    sqdB = Buf("sqd")
                    wst = sb(s2, "wst", [128, 32], F32)
                    wstB = [Buf("wst0"), Buf("wst1")]
                    Y0s, Y1s = scrA[:, 0:512], scrA[:, 512:1024]
                    Rcat = scrA[0:64, 1024:1536]
                    sdd, rr = scrA[:, 1536:2048], scrA[:, 2048:2560]
                    Y0B, Y1B, RcB, sddB, rrB = Buf("Y0s"), Buf("Y1s"), Buf("Rcat"), Buf("sdd"), Buf("rr")
                    sctr = [0]

                    for qt in range(4):
                        def dest_q(tp, nch, tb, tpb):
                            tl = tb % 4
                            act(lambda e: e.activation(out=QTv[:, :, tl * 128:(tl + 1) * 128], in_=tp, func=AF.Copy),
                                reads=[PB[tpb]], writes=[QTB])
                        run_pipeline([qt * 4 + tl for tl in range(4)], wq_v, wB,
                                     [[(0, 0, 512), (1, 512, 512)], [(2, 0, 512), (3, 512, 512)]],
                                     [7, 5], [6, 4], 1024, [(0, 8, 0), (8, 8, 2)], dest_q, None, tq_off=1)
                        T.barrier()
                        items = [(nl, g) for nl in range(4) for g in range(2)]

                        def win_A(idx):
                            nl, g = items[idx]
                            n = qt * 4 + nl
                            cs = [c for c in (n - 1, n, n + 1) if 0 <= c < NB]
                            for ci, c in enumerate(cs):
                                pair = sctr[0] % 2
                                sctr[0] += 1
                                b0 = 2 * pair
                                E = Ew[(idx % 2) * 3 + ci]
                                EB = EwB[(idx % 2) * 3 + ci]

                                def emit_s(pe, c=c, b0=b0, g=g, nl=nl):
                                    k0 = 4 if g == 0 else 5
                                    k1 = 4 if g == 1 else 5
                                    pe.matmul(PS[b0][:, 0:256].rearrange("p (a q) -> p a q", a=2),
                                              lhsT=KTv[0:64, k0, c * 128:(c + 1) * 128],
                                              rhs=QTv[0:64, 2 * g:2 * g + 2, nl * 128:(nl + 1) * 128],
                                              start=True, stop=True)
                                    return pe.matmul(PS[b0 + 1][:, 0:256].rearrange("p (a q) -> p a q", a=2),
                                                     lhsT=KTv[64:128, k1, c * 128:(c + 1) * 128],
                                                     rhs=QTv[64:128, 2 * g:2 * g + 2, nl * 128:(nl + 1) * 128],
                                                     start=True, stop=True)
                                pe_group(emit_s, reads=[KTB, QTB], writes=[PB[b0], PB[b0 + 1]])
                                act(lambda e, E=E, b0=b0: e.activation(
                                    out=E.rearrange("p (a q) -> p a q", a=2),
                                    in_=PSbig[:, b0 * 512:(b0 + 2) * 512].rearrange("p (a q) -> p a q", a=2)[:, :, 0:256],
                                    func=AF.Exp, scale=0.125), reads=[PB[b0], PB[b0 + 1]], writes=[EB])
                                if c != n:
                                    m = mge if c == n - 1 else mle
                                    dve(lambda e, E=E, m=m: e.tensor_tensor(
                                        out=E.rearrange("p (a q) -> p a q", a=4),
                                        in0=E.rearrange("p (a q) -> p a q", a=4),
                                        in1=m[:].unsqueeze(1).to_broadcast([128, 4, 128]), op=ALU.mult),
                                        reads=[EB, constB], writes=[EB])

                        def win_B(idx):
                            nl, g = items[idx]
                            n = qt * 4 + nl
                            cs = [c for c in (n - 1, n, n + 1) if 0 <= c < NB]
                            sp = idx % 2
                            ob = 4 + sp
                            Ov = PS[ob][:, 0:260].rearrange("p (h d) -> p h d", h=4)
                            Es = [Ew[sp * 3 + ci] for ci in range(len(cs))]

                            def emit_pv(pe):
                                ins = None
                                for hd in range(4):
                                    half, ci2 = hd % 2, hd // 2
                                    off = half * 256 + ci2 * 128
                                    for ci, c in enumerate(cs):
                                        ins = pe.matmul(Ov[:, hd, :], lhsT=Es[ci][:, off:off + 128],
                                                        rhs=vaugv[:, c, g, 0:65], start=(ci == 0),
                                                        stop=(ci == len(cs) - 1))
                                return ins
                            pe_group(emit_pv, reads=[EwB[sp * 3 + ci] for ci in range(len(cs))] + [VB], writes=[PB[ob]])
                            zz = wst[:, sp * 8:sp * 8 + 4]
                            rz = wst[:, 16 + sp * 8:16 + sp * 8 + 4]
                            dve(lambda e: e.tensor_tensor(out=zz, in0=Ov[:, :, 64], in1=esink[:, 4 * g:4 * g + 4], op=ALU.add),
                                reads=[PB[ob], miscB], writes=[wstB[sp]])
                            dve(lambda e: e.reciprocal(out=rz, in_=zz), reads=[wstB[sp]], writes=[wstB[sp]])
                            dve(lambda e: e.tensor_tensor(
                                out=yav[:, nl, g * 256:(g + 1) * 256].rearrange("p (h d) -> p h d", h=4),
                                in0=Ov[:, :, 0:64], in1=rz.unsqueeze(2).to_broadcast([128, 4, 64]),
                                op=ALU.mult), reads=[PB[ob], wstB[sp]], writes=[yaB])
                            if g == 1:
                                tpb = 6 + (nl % 2)
                                tp = ps_bf(tpb, 4)

                                def emit_ty(pe):
                                    ins = None
                                    for k in range(4):
                                        ins = pe.transpose(out=tp[:, k, :], in_=yav[:, nl, k * 128:(k + 1) * 128],
                                                           identity=ident[:])
                                    return ins
                                pe_group(emit_ty, reads=[yaB, constB], writes=[PB[tpb]])
                                act(lambda e: e.activation(out=yTv[:, 0:4, nl * 128:(nl + 1) * 128], in_=tp,
                                                           func=AF.Copy), reads=[PB[tpb]], writes=[yTB])

                        win_A(0)
                        for idx in range(len(items)):
                            if idx + 1 < len(items):
                                win_A(idx + 1)
                            win_B(idx)
                        T.barrier()
                        chunks = [(h, c) for h in range(4) for c in range(NB)]

                        def s_mm(i):
                            h, c = chunks[i]
                            b0 = 2 * (i % 2)

                            def emit(pe):
                                pe.matmul(PS[b0][:, :], lhsT=KTv[0:64, h, c * 128:(c + 1) * 128],
                                          rhs=QTv[0:64, 4 + h, :], start=True, stop=True)
                                return pe.matmul(PS[b0 + 1][:, :], lhsT=KTv[64:128, h, c * 128:(c + 1) * 128],
                                                 rhs=QTv[64:128, 4 + h, :], start=True, stop=True)
                            pe_group(emit, reads=[KTB, QTB], writes=[PB[b0], PB[b0 + 1]])

                        def part1(h):
                            y1b = 6 + (h % 2)
                            act(lambda e: e.activation(out=Y0s, in_=PS[4][:, :], func=AF.Copy), reads=[PB[4]], writes=[Y0B])
                            dve(lambda e: e.tensor_copy(out=Rcat, in_=PS[5][0:64, :]), reads=[PB[5]], writes=[RcB])
                            dve(lambda e: e.tensor_copy(out=Y1s, in_=PS[y1b][:, :]), reads=[PB[y1b]], writes=[Y1B])
                            dve(lambda e: e.reciprocal(out=Rcat, in_=Rcat), reads=[RcB], writes=[RcB])

                        def make_stages(h):
                            sbk = 6 + (h % 2)

                            def stA():
                                pe_group(lambda pe: pe.matmul(PS[sbk][:, :], lhsT=sel[:, 0:128], rhs=Rcat, start=True, stop=True),
                                         reads=[RcB, constB], writes=[PB[sbk]])
                                dve(lambda e: e.tensor_tensor(out=Y0s, in0=PS[sbk][:, :], in1=Y0s, op=ALU.mult),
                                    reads=[PB[sbk], Y0B], writes=[Y0B])

                            def stB():
                                pe_group(lambda pe: pe.matmul(PS[sbk][:, :], lhsT=sel[:, 128:256], rhs=Rcat, start=True, stop=True),
                                         reads=[RcB, constB], writes=[PB[sbk]])
                                dve(lambda e: e.tensor_tensor(out=Y1s, in0=PS[sbk][:, :], in1=Y1s, op=ALU.mult),
                                    reads=[PB[sbk], Y1B], writes=[Y1B])
                                dve(lambda e: e.scalar_tensor_tensor(out=Y0s, in0=Y1s, scalar=neglam, in1=Y0s, op0=ALU.mult,
                                                                     op1=ALU.add), reads=[Y0B, Y1B, miscB], writes=[Y0B])
                                dve(lambda e: e.tensor_tensor(out=sqd, in0=Y0s, in1=Y0s, op=ALU.mult), reads=[Y0B], writes=[sqdB])

                            def stC():
                                pe_group(lambda pe: pe.matmul(PS[sbk][:, :], lhsT=ones[:], rhs=sqd, start=True, stop=True),
                                         reads=[sqdB, constB], writes=[PB[sbk]])
                                act(lambda e: e.activation(out=sdd, in_=PS[sbk][:, :], func=AF.Ln, bias=eps_ap,
                                                           scale=1.0 / 128), reads=[PB[sbk], miscB], writes=[sddB])

                            def stD():
                                act(lambda e: e.activation(out=rr, in_=sdd, func=AF.Exp, scale=-0.5),
                                    reads=[sddB], writes=[rrB])
                                dve(lambda e: e.scalar_tensor_tensor(out=yTv[:, 4 + h, :], in0=Y0s, scalar=gsub, in1=rr,
                                                                     op0=ALU.mult, op1=ALU.mult),
                                    reads=[Y0B, rrB, miscB], writes=[yTB])
                            return [stA, stB, stC, stD]

                        pending = []
                        s_mm(0)
                        for i, (h, c) in enumerate(chunks):
                            par = i % 2
                            if i + 1 < len(chunks):
                                s_mm(i + 1)
                            act(lambda e, par=par: e.activation(out=Ed[par], in_=PSbig[:, 2 * par * 512:(2 * par + 2) * 512],
                                                                func=AF.Exp, scale=0.125),
                                reads=[PB[2 * par], PB[2 * par + 1]], writes=[EdB[par]])
                            if c == 0 and h > 0:
                                part1(h - 1)
                                pending = make_stages(h - 1)
                            y1b = 6 + (h % 2)

                            def emit_pv(pe, c=c, par=par, h=h, y1b=y1b):
                                st, sp_ = (c == 0), (c == NB - 1)
                                pe.matmul(PS[4][:, :], lhsT=vbv[:, c, h * 128:(h + 1) * 128], rhs=Ed[par][:, 0:512],
                                          start=st, stop=sp_)
                                pe.matmul(PS[y1b][:, :], lhsT=vbv[:, c, h * 128:(h + 1) * 128], rhs=Ed[par][:, 512:1024],
                                          start=st, stop=sp_)
                                pe.matmul(PS[5][0:32, :], lhsT=ones[:, 0:32], rhs=Ed[par][:, 0:512], start=st, stop=sp_)
                                return pe.matmul(PS[5][32:64, :], lhsT=ones[:, 0:32], rhs=Ed[par][:, 512:1024],
                                                 start=st, stop=sp_)
                            pe_group(emit_pv, reads=[EdB[par], VB, constB], writes=[PB[4], PB[5], PB[y1b]])
                            if pending and c in (4, 7, 10, 12):
                                pending.pop(0)()
                        part1(3)
                        for f in make_stages(3):
                            f()
                        T.barrier()
                        for tl in range(4):
                            tb = qt * 4 + tl
                            b0, b1 = 2 * (tl % 2), 2 * (tl % 2) + 1

                            def emit_o(pe, tl=tl, b0=b0, b1=b1):
                                ins = None
                                for k in range(8):
                                    pe.matmul(PS[b0][:, :], lhsT=yTv[:, k, tl * 128:(tl + 1) * 128],
                                              rhs=wo_v[:, k, 0:512], start=(k == 0), stop=(k == 7))
                                    ins = pe.matmul(PS[b1][:, :], lhsT=yTv[:, k, tl * 128:(tl + 1) * 128],
                                                    rhs=wo_v[:, k, 512:1024], start=(k == 0), stop=(k == 7))
                                return ins
                            pe_group(emit_o, reads=[yTB, woB], writes=[PB[b0], PB[b1]])
                            for hf, bk in ((0, b0), (1, b1)):
                                dve(lambda e, tb=tb, hf=hf, bk=bk: e.tensor_tensor(
                                    out=Xv[:, tb, hf * 512:(hf + 1) * 512], in0=PS[bk][:, :],
                                    in1=Xv[:, tb, hf * 512:(hf + 1) * 512], op=ALU.add),
                                    reads=[PB[bk], XB[tb]], writes=[XB[tb]])
                            stat_of(tb, hb[0][:], hbB[0])
                        T.barrier()

            with ExitStack() as sf:
                hTa = sb(sf, "hTa", [128, 8 * S], BF16)
                hTav = hTa[:].rearrange("p (k t) -> p k t", k=8)
                hTaB = Buf("hTa")
                uT = sb(sf, "uT", [128, 6 * S], BF16)
                uTv = uT[:].rearrange("p (j t) -> p j t", j=6)
                uTB = Buf("uT")
                wdn = [sb(sf, f"wdn{i}", [128, 6 * D], BF16) for i in range(2)]
                wdnv = [t[:].rearrange("p (j c) -> p j c", j=6) for t in wdn]
                wdnB = [Buf("wdn0"), Buf("wdn1")]
                wup = [sb(sf, f"wup{i}", [128, 8 * 256], BF16) for i in range(3)]
                wupv = [t[:].rearrange("p (k c) -> p k c", k=8) for t in wup]
                wupB = [Buf(f"wup{i}") for i in range(3)]
                graw = sb(sf, "graw", [128, S + 2], F32)
                grawB = Buf("graw")
                c1 = sb(sf, "c1", [128, S], F32)
                c1B = Buf("c1")
                val = sb(sf, "val", [128, S], F32)
                valB = Buf("val")
                hbf = sb(sf, "hbf", [128, D], BF16)
                hbfB = Buf("hbf")
                cw = sm(l, O_CW, 66).rearrange("p (j t) -> p j t", t=3)
                cb = sm(l, O_CB, 22)
                dsrc = wdn_d[l].rearrange("p (j c) -> p j c", j=NJ)

                def load_up(j):
                    s = j % 3
                    T.dma("pool", sU[s], out=wup[s][:], in_=wup_d[l, j], writes=[wupB[s]], max_dma_last_dim=4096)

                def load_dn(gi):
                    grp = FF_GROUPS[gi]
                    s = gi % 2
                    T.dma("pool", sD[s], out=wdnv[s][:, 0:len(grp), :], in_=dsrc[:, grp[0]:grp[0] + len(grp), :],
                          writes=[wdnB[s]], max_dma_last_dim=4096)

                load_up(0)
                load_up(1)
                load_dn(0)
                load_dn(1)
                pool(lambda e: e.memset(graw[:, 0:1], 0.0), writes=[grawB])
                pool(lambda e: e.memset(graw[:, S + 1:S + 2], 0.0), writes=[grawB])
                hbf2 = [hbf, sb(sf, "hbf2", [128, D], BF16)]
                hbf2B = [hbfB, Buf("hbf2")]

                def f_hb(tb):
                    pr = tb % 2
                    act(lambda e: e.activation(out=hbf2[pr][:], in_=Xv[:, tb, :], func=AF.Copy, scale=rstd[:, tb:tb + 1]),
                        reads=[XB[tb], rstdBs[tb]], writes=[hbf2B[pr]])

                def f_T(tb):
                    pr = tb % 2
                    bank = 6 + pr
                    tp = ps_bf(bank, 8)

                    def emit(pe):
                        ins = None
                        for k in range(8):
                            ins = pe.transpose(out=tp[:, k, :], in_=hbf2[pr][:, k * 128:(k + 1) * 128], identity=ident[:])
                        return ins
                    pe_group(emit, reads=[hbf2B[pr], constB], writes=[PB[bank]])
                    dve(lambda e: e.tensor_tensor(out=hTav[:, :, tb * 128:(tb + 1) * 128], in0=tp,
                                                  in1=gF.unsqueeze(2).to_broadcast([128, 8, 128]), op=ALU.mult),
                        reads=[PB[bank], constB], writes=[hTaB])
                f_hb(0)
                for tb in range(NB):
                    if tb + 1 < NB:
                        f_hb(tb + 1)
                    f_T(tb)

                slot = [0]

                def next_slot():
                    s_ = slot[0] % 4
                    slot[0] += 1
                    return 2 * s_, 2 * s_ + 1

                def up_jobs(j):
                    s = j % 3
                    for pr in range(2):
                        tts = (2 * pr, 2 * pr + 1)
                        for which in range(2):
                            b0, b1 = next_slot()
                            c0 = which * 128

                            def emit(pe, b0=b0, b1=b1, c0=c0, tts=tts):
                                ins = None
                                for k in range(8):
                                    pe.matmul(PS[b0][:, :], lhsT=wupv[s][:, k, c0:c0 + 128],
                                              rhs=hTav[:, k, tts[0] * 512:(tts[0] + 1) * 512], start=(k == 0), stop=(k == 7))
                                    ins = pe.matmul(PS[b1][:, :], lhsT=wupv[s][:, k, c0:c0 + 128],
                                                    rhs=hTav[:, k, tts[1] * 512:(tts[1] + 1) * 512], start=(k == 0),
                                                    stop=(k == 7))
                                return ins
                            pe_group(emit, reads=[wupB[s], hTaB], writes=[PB[b0], PB[b1]])
                            for bk, tt in ((b0, tts[0]), (b1, tts[1])):
                                if which == 0:
                                    act(lambda e, tt=tt, bk=bk: e.activation(out=graw[:, 1 + tt * 512:1 + (tt + 1) * 512],
                                                                             in_=PS[bk][:, :], func=AF.Copy),
                                        reads=[PB[bk]], writes=[grawB])
                                    act(lambda e, tt=tt, bk=bk: e.activation(out=c1[:, tt * 512:(tt + 1) * 512],
                                                                             in_=PS[bk][:, :], func=AF.Identity,
                                                                             scale=cw[:, j, 1:2], bias=cb[:, j:j + 1]),
                                        reads=[PB[bk], constB], writes=[c1B])
                                else:
                                    act(lambda e, tt=tt, bk=bk: e.activation(out=val[:, tt * 512:(tt + 1) * 512],
                                                                             in_=PS[bk][:, :], func=AF.Copy),
                                        reads=[PB[bk]], writes=[valB])

                def chain_a(j):
                    dve(lambda e: e.scalar_tensor_tensor(out=c1[:], in0=graw[:, 0:S], scalar=cw[:, j, 0:1],
                                                         in1=c1[:], op0=ALU.mult, op1=ALU.add),
                        reads=[grawB, c1B, constB], writes=[c1B])
                    dve(lambda e: e.scalar_tensor_tensor(out=c1[:], in0=graw[:, 2:S + 2], scalar=cw[:, j, 2:3],
                                                         in1=c1[:], op0=ALU.mult, op1=ALU.add),
                        reads=[grawB, c1B, constB], writes=[c1B])
                    act(lambda e: e.activation(out=c1[:], in_=c1[:], func=AF.Silu), reads=[c1B], writes=[c1B])

                def chain_b(jl):
                    dve(lambda e: e.tensor_tensor(out=uTv[:, jl, :], in0=c1[:], in1=val[:], op=ALU.mult),
                        reads=[c1B, valB], writes=[uTB])

                def down(gi):
                    grp = FF_GROUPS[gi]
                    s = gi % 2
                    for tb in range(NB):
                        b0, b1 = next_slot()

                        def emit_d(pe, tb=tb, b0=b0, b1=b1):
                            ins = None
                            for jl in range(len(grp)):
                                pe.matmul(PS[b0][:, :], lhsT=uTv[:, jl, tb * 128:(tb + 1) * 128],
                                          rhs=wdnv[s][:, jl, 0:512], start=(jl == 0), stop=(jl == len(grp) - 1))
                                ins = pe.matmul(PS[b1][:, :], lhsT=uTv[:, jl, tb * 128:(tb + 1) * 128],
                                                rhs=wdnv[s][:, jl, 512:1024], start=(jl == 0), stop=(jl == len(grp) - 1))
                            return ins
                        pe_group(emit_d, reads=[uTB, wdnB[s]], writes=[PB[b0], PB[b1]])
                        for hf, bk in ((0, b0), (1, b1)):
                            dve(lambda e, tb=tb, hf=hf, bk=bk: e.tensor_tensor(
                                out=Xv[:, tb, hf * 512:(hf + 1) * 512], in0=PS[bk][:, :],
                                in1=Xv[:, tb, hf * 512:(hf + 1) * 512], op=ALU.add),
                                reads=[PB[bk], XB[tb]], writes=[XB[tb]])
                        if gi == len(FF_GROUPS) - 1 and l != layers[-1]:
                            stat_of(tb, hbf[:], hbfB)
                        if gi == len(FF_GROUPS) - 1 and l == layers[-1]:
                            T.dma("sp", sX[tb], out=yv_d[:, tb, :], in_=Xv[:, tb, :], reads=[XB[tb]])

                prev = None
                for gi, grp in enumerate(FF_GROUPS):
                    for jl, j in enumerate(grp):
                        if j + 2 < NJ:
                            load_up(j + 2)
                        up_jobs(j)
                        chain_a(j)
                        if jl == 0 and prev is not None:
                            down(prev)
                            if gi + 1 < len(FF_GROUPS):
                                load_dn(gi + 1)
                        chain_b(jl)
                    prev = gi
                down(prev)
                T.barrier()

        T.barrier()
    return nc


def _prep_shared(inp):
    f = lambda a: np.ascontiguousarray(np.asarray(a, dtype=np.float32))
    w_in = f(inp["w_in"])
    L = w_in.shape[0]
    qa, ka, va = w_in[:, :, 0:512], w_in[:, :, 512:640], w_in[:, :, 640:768]
    qb, kb, vb = w_in[:, :, 768:1280], w_in[:, :, 1280:1792], w_in[:, :, 1792:2304]
    kaS = np.concatenate([ka[:, :, 64:128], ka[:, :, 0:64]], axis=2)
    wkv = np.concatenate([kb, ka, kaS, va, vb], axis=2)
    wq = np.concatenate([qa, qb], axis=2)

    def pk(w):
        Lc, K, C = w.shape
        return np.ascontiguousarray(w.reshape(Lc, K // 128, 128, C).transpose(0, 2, 1, 3).reshape(Lc, 128, -1))
    w_up = f(inp["w_up"])
    gate, val = w_up[:, :, :DFF], w_up[:, :, DFF:]
    gv = np.concatenate([gate.reshape(L, D, NJ, 1, 128), val.reshape(L, D, NJ, 1, 128)], axis=3)
    wup = np.ascontiguousarray(gv.reshape(L, 8, 128, NJ, 256).transpose(0, 3, 2, 1, 4).reshape(L, NJ, 128, 8 * 256))
    small = np.zeros((128, L, NS), np.float32)
    for l in range(L):
        small[:, l, O_GA:O_GA + 8] = f(inp["g_attn"])[l].reshape(8, 128).T
        small[:, l, O_GF:O_GF + 8] = f(inp["g_ffn"])[l].reshape(8, 128).T
        small[:, l, O_QKG:O_QKG + 256] = np.concatenate([f(inp["qn_a"])[l], f(inp["kn_a"])[l], f(inp["qn_b"])[l],
                                                         f(inp["kn_b"])[l]])[None, :]
        small[:, l, O_SINK:O_SINK + 8] = f(inp["sink"])[l][None, :]
        small[:, l, O_LAM:O_LAM + 256] = np.concatenate([f(inp["lq1"])[l], f(inp["lk1"])[l], f(inp["lq2"])[l],
                                                         f(inp["lk2"])[l]])[None, :]
        small[:, l, O_SUB] = f(inp["subln"])[l]
        small[:, l, O_CW:O_CW + 66] = f(inp["conv_w"])[l].reshape(3, NJ, 128).transpose(2, 1, 0).reshape(128, 66)
        small[:, l, O_CB:O_CB + 22] = f(inp["conv_b"])[l].reshape(NJ, 128).T
    inv = (1.0 / (np.float32(10000.0) ** (np.arange(0, 64, 2, dtype=np.float32) / np.float32(64)))).astype(np.float32)
    ang = (np.arange(S, dtype=np.float32)[:, None] * inv[None, :]).astype(np.float32)
    cos = np.cos(ang).astype(np.float32).reshape(NB, 128, 32).transpose(1, 0, 2)
    sin = np.sin(ang).astype(np.float32).reshape(NB, 128, 32).transpose(1, 0, 2)
    sinS = np.stack([sin, -sin], axis=2)
    return {
        "cosT": np.ascontiguousarray(cos.reshape(128, NB * 32)),
        "sinS": np.ascontiguousarray(sinS.reshape(128, NB * 64)),
        "small": np.ascontiguousarray(small.reshape(128, L * NS)),
        "wkv": pk(wkv), "wq": pk(wq), "wo": pk(f(inp["w_out"])), "wup": wup,
        "wdn": np.ascontiguousarray(f(inp["w_down"]).reshape(L, NJ, 128, D).transpose(0, 2, 1, 3).reshape(L, 128, NJ * D)),
    }


_NC_CACHE = {}


def _get_nc(layers):
    key = tuple(layers)
    if key not in _NC_CACHE:
        _NC_CACHE[key] = build(layers)
    return _NC_CACHE[key]


def kernel(**inputs):
    x = np.ascontiguousarray(np.asarray(inputs["x"], dtype=np.float32))
    shared = _prep_shared(inputs)
    nc = _get_nc((0, 1))
    in_maps = [dict(shared, x=x[b]) for b in range(8)]
    res = run_bass_kernel_spmd(nc, in_maps, core_ids=list(range(8)))
    return np.stack([np.asarray(r["y"], dtype=np.float32) for r in res.results], axis=0)
```

```python
import math
import numpy as np
from contextlib import ExitStack
import concourse.bass as bass
import concourse.mybir as mybir
from concourse.bass_utils import run_bass_kernel_spmd

F32 = mybir.dt.float32
BF16 = mybir.dt.bfloat16
AF = mybir.ActivationFunctionType
ALU = mybir.AluOpType
AX = mybir.AxisListType

S = 2048
D = 1024
NB = 16
DFF = 2816
NJ = 22
L_TOTAL = 2
EPS = 1e-6
O_GA, O_GF, O_QKG, O_SINK, O_LAM, O_SUB, O_CW, O_CB = 0, 8, 16, 272, 280, 536, 537, 603
NS = 625
KV_COLS = 1408
Q_COLS = 1024
FF_GROUPS = [list(range(0, 6)), list(range(6, 11)), list(range(11, 17)), list(range(17, 22))]


class Buf:
    __slots__ = ("name", "w", "r")

    def __init__(self, name=""):
        self.name = name
        self.w = {}
        self.r = {}


class Slot:
    def __init__(self, sem):
        self.sem = sem
        self.count = 0


class Trk:
    def __init__(self, nc, es):
        self.nc = nc
        self.es = es
        self.engs = {"pe": nc.tensor, "act": nc.scalar, "dve": nc.vector, "pool": nc.gpsimd, "sp": nc.sync}
        self.sems = []
        self.cur = {}
        self.esem = {}
        for e in self.engs:
            self.esem[e] = self.new_sem("s_" + e)
        self.tick = {e: 0 for e in self.engs}
        self.known = {e: {} for e in self.engs}

    def new_sem(self, name):
        s = self.es.enter_context(self.nc.semaphore(name))
        self.sems.append(s)
        self.cur[len(self.sems) - 1] = 0
        return len(self.sems) - 1

    def new_slot(self, name):
        return Slot(self.new_sem(name))

    def wait(self, e, reads=(), writes=()):
        own = self.esem[e]
        need = {}
        for b in reads:
            for s, v in b.w.items():
                if need.get(s, 0) < v:
                    need[s] = v
        pe_own = own if e == "pe" else -1
        for b in writes:
            for s, v in b.w.items():
                if s != pe_own and need.get(s, 0) < v:
                    need[s] = v
            for s, v in b.r.items():
                if need.get(s, 0) < v:
                    need[s] = v
        k = self.known[e]
        for s, v in need.items():
            if k.get(s, 0) < v:
                self.engs[e].wait_ge(self.sems[s], v)
                k[s] = v

    def done(self, e, ins, reads=(), writes=()):
        self.tick[e] += 1
        own = self.esem[e]
        ins.then_inc(self.sems[own], 1)
        t = self.tick[e]
        self.cur[own] = t
        for b in writes:
            b.w[own] = t
        for b in reads:
            b.r[own] = t

    def op(self, e, fn, reads=(), writes=()):
        self.wait(e, reads, writes)
        ins = fn(self.engs[e])
        self.done(e, ins, reads, writes)
        return ins

    def dma(self, q, slot, out, in_, reads=(), writes=(), **kw):
        self.wait(q, reads, writes)
        ins = self.engs[q].dma_start(out=out, in_=in_, **kw)
        slot.count += 16
        ins.then_inc(self.sems[slot.sem], 16)
        self.cur[slot.sem] = slot.count
        for b in writes:
            b.w[slot.sem] = slot.count
        for b in reads:
            b.r[slot.sem] = slot.count

    def barrier(self, engines=("pe", "act", "dve", "pool", "sp")):
        for e in engines:
            k = self.known[e]
            for s, v in self.cur.items():
                if v > 0 and k.get(s, 0) < v:
                    self.engs[e].wait_ge(self.sems[s], v)
                    k[s] = v


def build(layers=(0, 1), first=True, last=True):
    nc = bass.Bass("TRN2", target_bir_lowering=False)

    def dram(name, shape, kind="ExternalInput"):
        return nc.dram_tensor(name, shape, F32, kind=kind).ap()

    x_d = dram("x", [S, D])
    y_d = dram("y", [S, D], kind="ExternalOutput")
    cos_d = dram("cosT", [128, NB * 32])
    sin_d = dram("sinS", [128, NB * 64])
    small_d = dram("small", [128, L_TOTAL * NS])
    wkv_d = dram("wkv", [L_TOTAL, 128, 8 * KV_COLS])
    wq_d = dram("wq", [L_TOTAL, 128, 8 * Q_COLS])
    wo_d = dram("wo", [L_TOTAL, 128, 8 * D])
    wup_d = dram("wup", [L_TOTAL, NJ, 128, 8 * 256])
    wdn_d = dram("wdn", [L_TOTAL, 128, NJ * D])

    with ExitStack() as es:
        T = Trk(nc, es)

        _cnt = [0]

        def sb(stack, name, shape, dt):
            _cnt[0] += 1
            return stack.enter_context(nc.sbuf_tensor(f"{name}_{_cnt[0]}", shape, dt))

        def act(fn, reads=(), writes=()):
            return T.op("act", fn, reads, writes)

        def dve(fn, reads=(), writes=()):
            return T.op("dve", fn, reads, writes)

        def pool(fn, reads=(), writes=()):
            return T.op("pool", fn, reads, writes)

        def pe_group(emit, reads=(), writes=()):
            T.wait("pe", reads, writes)
            lastins = emit(nc.tensor)
            T.done("pe", lastins, reads, writes)

        X = sb(es, "X", [128, NB * D], F32)
        Xv = X[:].rearrange("p (b d) -> p b d", b=NB)
        XB = [Buf(f"X{i}") for i in range(NB)]
        ident = sb(es, "ident", [128, 128], BF16)
        ones = sb(es, "ones", [128, 128], BF16)
        mge = sb(es, "mge", [128, 128], BF16)
        mle = sb(es, "mle", [128, 128], BF16)
        cosT = sb(es, "cosT_s", [128, NB * 32], F32)
        sinS = sb(es, "sinS_s", [128, NB * 64], F32)
        cosv = cosT[:].rearrange("p (b i) -> p b i", b=NB)
        sinv = sinS[:].rearrange("p (b t i) -> p b t i", b=NB, t=2)
        small = sb(es, "small_s", [128, L_TOTAL * NS], F32)
        misc = sb(es, "misc", [128, 64], F32)
        sel = sb(es, "sel", [64, 256], F32)
        constB = Buf("const")
        miscB = Buf("misc")
        rstd = sb(es, "rstd", [128, 16], F32)
        ssq = sb(es, "ssq", [128, 16], F32)
        sd16 = sb(es, "sd16", [128, 16], F32)
        rstdBs = [Buf(f"rstd{i}") for i in range(NB)]
        ssqBs = [Buf(f"ssq{i}") for i in range(NB)]
        eps_ap = misc[:, 0:1]
        neglam = misc[:, 1:2]
        gsub = misc[:, 2:3]
        esink = misc[:, 3:11]

        PSbig = es.enter_context(nc.psum_tensor("psbig", [128, 4096], F32))
        PS = [PSbig[:, i * 512:(i + 1) * 512] for i in range(8)]
        PB = [Buf(f"ps{i}") for i in range(8)]

        def ps_bf(i, k):
            return PS[i][:].bitcast(BF16).rearrange("p (k c) -> p k c", c=128)[:, 0:k, :]

        sX = [T.new_slot(f"sx{i}") for i in range(NB)]
        sW = [T.new_slot(f"sw{i}") for i in range(8)]
        sU = [T.new_slot(f"su{i}") for i in range(3)]
        sD = [T.new_slot(f"sd{i}") for i in range(2)]
        sP = T.new_slot("sp0")

        xv_d = x_d.rearrange("(b p) d -> p b d", p=128)
        yv_d = y_d.rearrange("(b p) d -> p b d", p=128)
        T.dma("sp", sP, out=small[:], in_=small_d[:, :], writes=[constB])
        T.dma("sp", sP, out=cosT[:], in_=cos_d[:, :], writes=[constB])
        T.dma("sp", sP, out=sinS[:], in_=sin_d[:, :], writes=[constB])
        with ExitStack() as s0:
            tmpf = sb(s0, "tmpf", [128, 128], F32)
            tmpB = Buf("tmpf")
            pool(lambda e: e.memset(tmpf[:], 0.0), writes=[tmpB])
            pool(lambda e: e.affine_select(out=tmpf[:], in_=tmpf[:], pattern=[[-1, 128]], compare_op=ALU.not_equal,
                                           fill=1.0, base=0, channel_multiplier=1), reads=[tmpB], writes=[tmpB])
            dve(lambda e: e.tensor_copy(out=ident[:], in_=tmpf[:]), reads=[tmpB], writes=[constB])
            pool(lambda e: e.memset(tmpf[:], 1.0), reads=[tmpB], writes=[tmpB])
            dve(lambda e: e.tensor_copy(out=ones[:], in_=tmpf[:]), reads=[tmpB], writes=[constB])
            pool(lambda e: e.affine_select(out=tmpf[:], in_=tmpf[:], pattern=[[-1, 128]], compare_op=ALU.is_ge,
                                           fill=0.0, base=0, channel_multiplier=1), reads=[tmpB], writes=[tmpB])
            dve(lambda e: e.tensor_copy(out=mge[:], in_=tmpf[:]), reads=[tmpB], writes=[constB])
            pool(lambda e: e.memset(tmpf[:], 1.0), reads=[tmpB], writes=[tmpB])
            pool(lambda e: e.affine_select(out=tmpf[:], in_=tmpf[:], pattern=[[1, 128]], compare_op=ALU.is_ge,
                                           fill=0.0, base=0, channel_multiplier=-1), reads=[tmpB], writes=[tmpB])
            dve(lambda e: e.tensor_copy(out=mle[:], in_=tmpf[:]), reads=[tmpB], writes=[constB])
            pool(lambda e: e.memset(sel[:], 0.0), writes=[constB])
            pool(lambda e: e.memset(sel[0:1, 0:128], 1.0), reads=[constB], writes=[constB])
            pool(lambda e: e.memset(sel[32:33, 128:256], 1.0), reads=[constB], writes=[constB])
            pool(lambda e: e.memset(misc[:], 0.0), writes=[miscB])
            pool(lambda e: e.memset(misc[:, 0:1], EPS), reads=[miscB], writes=[miscB])
            T.barrier()

        def sm(l, off, n):
            return small[:, l * NS + off: l * NS + off + n]

        def stat_of(tb, junk, junkB):
            act(lambda e: e.activation(out=junk, in_=Xv[:, tb, :], func=AF.Square, accum_out=ssq[:, tb:tb + 1]),
                reads=[XB[tb]], writes=[junkB, ssqBs[tb]])
            act(lambda e: e.activation(out=sd16[:, tb:tb + 1], in_=ssq[:, tb:tb + 1], func=AF.Sqrt, bias=eps_ap,
                                       scale=1.0 / D), reads=[ssqBs[tb], miscB], writes=[ssqBs[tb]])
            dve(lambda e: e.reciprocal(out=rstd[:, tb:tb + 1], in_=sd16[:, tb:tb + 1]), reads=[ssqBs[tb]],
                writes=[rstdBs[tb]])

        def make_hT(tb, g_ap, hb, hbB, hT_out, hTB, bank):
            act(lambda e: e.activation(out=hb[:], in_=Xv[:, tb, :], func=AF.Copy, scale=rstd[:, tb:tb + 1]),
                reads=[XB[tb], rstdBs[tb]], writes=[hbB])
            tp = ps_bf(bank, 8)

            def emit(pe):
                ins = None
                for k in range(8):
                    ins = pe.transpose(out=tp[:, k, :], in_=hb[:, k * 128:(k + 1) * 128], identity=ident[:])
                return ins
            pe_group(emit, reads=[hbB, constB], writes=[PB[bank]])
            dve(lambda e: e.tensor_tensor(out=hT_out, in0=tp, in1=g_ap.unsqueeze(2).to_broadcast([128, 8, 128]),
                                          op=ALU.mult), reads=[PB[bank], constB], writes=[hTB])

        for l in layers:
            lam_init = 0.8 - 0.6 * math.exp(-0.3 * l)
            def small_params(sa_stack, l=l, lam_init=lam_init):
                if True:
                    j256 = sb(sa_stack, "j256", [128, 64], F32)
                    jB = Buf("j256")
                    lamv = sm(l, O_LAM, 256).rearrange("p (t d) -> p t d", t=4)
                    dve(lambda e: e.scalar_tensor_tensor(out=j256[:], in0=lamv[:, 0, :], scalar=1.0, in1=lamv[:, 1, :],
                                                         op0=ALU.mult, op1=ALU.mult, accum_out=misc[:, 11:12]),
                        reads=[constB, miscB], writes=[jB, miscB])
                    dve(lambda e: e.scalar_tensor_tensor(out=j256[:], in0=lamv[:, 2, :], scalar=1.0, in1=lamv[:, 3, :],
                                                         op0=ALU.mult, op1=ALU.mult, accum_out=misc[:, 12:13]),
                        reads=[constB, miscB], writes=[jB, miscB])
                    act(lambda e: e.activation(out=misc[:, 13:15], in_=misc[:, 11:13], func=AF.Exp),
                        reads=[miscB], writes=[miscB])
                    act(lambda e: e.activation(out=esink, in_=sm(l, O_SINK, 8), func=AF.Exp),
                        reads=[constB, miscB], writes=[miscB])
                    dve(lambda e: e.tensor_tensor(out=neglam, in0=misc[:, 14:15], in1=misc[:, 13:14], op=ALU.subtract),
                        reads=[miscB], writes=[miscB])
                    dve(lambda e: e.tensor_scalar(out=neglam, in0=neglam, scalar1=-lam_init, scalar2=None, op0=ALU.add),
                        reads=[miscB], writes=[miscB])
                    dve(lambda e: e.tensor_scalar(out=gsub, in0=sm(l, O_SUB, 1), scalar1=1.0 - lam_init, scalar2=None,
                                                  op0=ALU.mult), reads=[constB, miscB], writes=[miscB])

            qkg = sm(l, O_QKG, 256).rearrange("p (t d) -> p t d", t=4)
            gA = sm(l, O_GA, 8)
            gF = sm(l, O_GF, 8)

            with ExitStack() as sa:
                KT = sb(sa, "KT", [128, 6 * S], BF16)
                KTv = KT[:].rearrange("p (k t) -> p k t", k=6)
                KTB = Buf("KT")
                vaug = sb(sa, "vaug", [128, NB * 2 * 66], BF16)
                vaugv = vaug[:].rearrange("p (b g d) -> p b g d", b=NB, g=2)
                vb = sb(sa, "vb", [128, NB * 512], BF16)
                vbv = vb[:].rearrange("p (b d) -> p b d", b=NB)
                VB = Buf("V")
                scrA = sb(sa, "scrA", [128, 3072], F32)
                arena2 = sb(sa, "arena2", [128, 3072], F32)
                scrs = [scrA, arena2]
                a2bf = arena2[:].bitcast(BF16)
                hb = [sb(sa, f"hb{i}", [128, D], BF16) for i in range(2)]
                hbB = [Buf("hb0"), Buf("hb1")]
                hT = [sb(sa, f"hT{i}", [128, 8 * 128], BF16) for i in range(2)]
                hTv = [t[:].rearrange("p (k c) -> p k c", k=8) for t in hT]
                hTB = [Buf("hT0"), Buf("hT1")]
                qtoks = [sb(sa, f"qtok{i}", [128, 1024], BF16) for i in range(2)]
                qtokBs = [Buf("qtok0"), Buf("qtok1")]
                hstats = [sb(sa, f"hstat{i}", [128, 64], F32) for i in range(2)]
                hstatBs = [Buf("hstat0"), Buf("hstat1")]
                pqBs, sqABs, BtBs = [Buf("pq0"), Buf("pq1")], [Buf("sqA0"), Buf("sqA1")], [Buf("Bt0"), Buf("Bt1")]

                pool(lambda e: e.memset(vaug[:], 1.0), writes=[VB])
                wq = sb(sa, "wq_s", [128, 8 * Q_COLS], BF16)
                wq_v = wq[:].rearrange("p (k c) -> p k c", k=8)
                wqB = Buf("wq")

                def run_pipeline(tbs, w_v, wB, bank_sets, tph_banks, tpq_banks, nqk, runs, dest_fn, v_fn, tq_off=0, pre_fn=None, ahead=False):
                    H = nqk // 64
                    nch = nqk // 128

                    def bufs(i):
                        par = i % 2
                        scr = scrs[par]
                        return dict(par=par, scr=scr, hstat=hstats[par], qtok=qtoks[par], pqB=pqBs[par], sqAB=sqABs[par],
                                    BtB=BtBs[par], qtokB=qtokBs[par], hstatB=hstatBs[par], pq=scr[:, 0:nqk],
                                    sqA=scr[:, 1024:1024 + nqk], Bt=scr[:, 2048:2048 + nqk])

                    def w_hb(i, tb):
                        par = i % 2
                        act(lambda e: e.activation(out=hb[par][:], in_=Xv[:, tb, :], func=AF.Copy, scale=rstd[:, tb:tb + 1]),
                            reads=[XB[tb], rstdBs[tb]], writes=[hbB[par]])

                    def w_th(i, tb):
                        par = i % 2
                        tp = ps_bf(tph_banks[par], 8)

                        def emit(pe):
                            ins = None
                            for k in range(8):
                                ins = pe.transpose(out=tp[:, k, :], in_=hb[par][:, k * 128:(k + 1) * 128], identity=ident[:])
                            return ins
                        pe_group(emit, reads=[hbB[par], constB], writes=[PB[tph_banks[par]]])

                    def w_hT(i, tb):
                        par = i % 2
                        tp = ps_bf(tph_banks[par], 8)
                        dve(lambda e: e.tensor_tensor(out=hTv[par], in0=tp, in1=gA.unsqueeze(2).to_broadcast([128, 8, 128]),
                                                      op=ALU.mult), reads=[PB[tph_banks[par]], constB], writes=[hTB[par]])

                    def w_proj(i, tb):
                        par = i % 2

                        def emit(pe):
                            ins = None
                            for k in range(8):
                                for (bk, c0, cn) in bank_sets[par]:
                                    ins = pe.matmul(PS[bk][:, 0:cn], lhsT=hTv[par][:, k, :], rhs=w_v[:, k, c0:c0 + cn],
                                                    start=(k == 0), stop=(k == 7))
                            return ins
                        pe_group(emit, reads=[hTB[par], wB], writes=[PB[bk] for (bk, _, _) in bank_sets[par]])

                    def w_evac(i, tb):
                        d = bufs(i)
                        for (bk, c0, cn) in bank_sets[d["par"]]:
                            if c0 >= nqk:
                                continue
                            n = min(cn, nqk - c0)
                            act(lambda e, bk=bk, c0=c0, n=n: e.activation(out=d["scr"][:, c0:c0 + n], in_=PS[bk][:, 0:n],
                                                                          func=AF.Copy),
                                reads=[PB[bk]], writes=[d["pqB"]])
                        if v_fn is not None:
                            v_fn(tb, bank_sets[d["par"]])
                        act(lambda e: e.activation(out=d["sqA"], in_=d["pq"], func=AF.Square), reads=[d["pqB"]],
                            writes=[d["sqAB"]])

                    def w_reduce(i, tb):
                        d = bufs(i)
                        dve(lambda e: e.tensor_reduce(out=d["hstat"][:, 0:H], in_=d["sqA"].rearrange("p (h d) -> p h d", d=64),
                                                      axis=AX.X, op=ALU.add), reads=[d["sqAB"]], writes=[d["hstatB"]])
                        pq3 = d["pq"].rearrange("p (h d) -> p h d", d=64)
                        for (h0, nh, gt) in runs:
                            pool(lambda e, h0=h0, nh=nh, gt=gt: e.tensor_tensor(
                                out=pq3[:, h0:h0 + nh, :], in0=pq3[:, h0:h0 + nh, :],
                                in1=qkg[:, gt, :].unsqueeze(1).to_broadcast([128, nh, 64]), op=ALU.mult),
                                reads=[d["pqB"], constB], writes=[d["pqB"]])

                    def w_sqrt(i, tb):
                        d = bufs(i)
                        hstat = d["hstat"]
                        act(lambda e: e.activation(out=hstat[:, 16:16 + H], in_=hstat[:, 0:H], func=AF.Sqrt, bias=eps_ap,
                                                   scale=1.0 / 64), reads=[d["hstatB"], miscB], writes=[d["hstatB"]])

                    def w_xn(i, tb):
                        d = bufs(i)
                        hstat = d["hstat"]
                        dve(lambda e: e.reciprocal(out=hstat[:, 32:32 + H], in_=hstat[:, 16:16 + H]),
                            reads=[d["hstatB"]], writes=[d["hstatB"]])
                        pq3 = d["pq"].rearrange("p (h d) -> p h d", d=64)
                        dve(lambda e: e.tensor_tensor(out=pq3, in0=pq3,
                                                      in1=hstat[:, 32:32 + H].unsqueeze(2).to_broadcast([128, H, 64]),
                                                      op=ALU.mult), reads=[d["pqB"], d["hstatB"]], writes=[d["pqB"]])

                    def w_AB(i, tb):
                        d = bufs(i)
                        pq4 = d["pq"].rearrange("p (h t i) -> p h t i", t=2, i=32)
                        A4 = d["sqA"].rearrange("p (h t i) -> p h t i", t=2, i=32)
                        B4 = d["Bt"].rearrange("p (h t i) -> p h t i", t=2, i=32)
                        Hh = H // 2
                        dve(lambda e: e.tensor_tensor(out=A4[:, 0:Hh], in0=pq4[:, 0:Hh],
                                                      in1=cosv[:, tb, :].unsqueeze(1).unsqueeze(1).to_broadcast([128, Hh, 2, 32]),
                                                      op=ALU.mult), reads=[d["pqB"], constB], writes=[d["sqAB"]])
                        pool(lambda e: e.tensor_tensor(out=A4[:, Hh:H], in0=pq4[:, Hh:H],
                                                       in1=cosv[:, tb, :].unsqueeze(1).unsqueeze(1).to_broadcast([128, H - Hh, 2, 32]),
                                                       op=ALU.mult), reads=[d["pqB"], constB], writes=[d["sqAB"]])
                        pool(lambda e: e.tensor_tensor(out=B4, in0=pq4,
                                                       in1=sinv[:, tb, :, :].unsqueeze(1).to_broadcast([128, H, 2, 32]),
                                                       op=ALU.mult), reads=[d["pqB"], constB], writes=[d["BtB"]])

                    def w_o(i, tb):
                        d = bufs(i)
                        A4 = d["sqA"].rearrange("p (h t i) -> p h t i", t=2, i=32)
                        B4 = d["Bt"].rearrange("p (h t i) -> p h t i", t=2, i=32)
                        o4 = d["qtok"][:, 0:nqk].rearrange("p (h t i) -> p h t i", t=2, i=32)
                        dve(lambda e: e.tensor_tensor(out=o4[:, :, 0, :], in0=A4[:, :, 0, :], in1=B4[:, :, 1, :], op=ALU.add),
                            reads=[d["sqAB"], d["BtB"]], writes=[d["qtokB"]])
                        dve(lambda e: e.tensor_tensor(out=o4[:, :, 1, :], in0=A4[:, :, 1, :], in1=B4[:, :, 0, :], op=ALU.add),
                            reads=[d["sqAB"], d["BtB"]], writes=[d["qtokB"]])

                    def w_tq(i, tb):
                        d = bufs(i)
                        tpb = tpq_banks[d["par"]]
                        tp = ps_bf(tpb, nch)

                        def emit_t(pe):
                            ins = None
                            for k in range(nch):
                                ins = pe.transpose(out=tp[:, k, :], in_=d["qtok"][:, k * 128:(k + 1) * 128], identity=ident[:])
                            return ins
                        pe_group(emit_t, reads=[d["qtokB"], constB], writes=[PB[tpb]])

                    def w_dest(i, tb):
                        tpb = tpq_banks[i % 2]
                        dest_fn(ps_bf(tpb, nch), nch, tb, tpb)

                    n = len(tbs)

                    def at(fn, idx):
                        if 0 <= idx < n:
                            fn(idx, tbs[idx])
                    ah = 1 if ahead else 0
                    if ahead:
                        at(w_hb, 0)
                        at(w_th, 0)
                        at(w_hT, 0)
                    for step in range(n + 2 + tq_off):
                        at(w_hb, step + ah)
                        if pre_fn is not None and step < n:
                            pre_fn(tbs[step])
                        at(w_evac, step - 1)
                        at(w_AB, step - 2)
                        at(w_th, step + ah)
                        if tq_off:
                            at(w_tq, step - 2 - tq_off)
                            at(w_dest, step - 2 - tq_off)
                        at(w_hT, step + ah)
                        at(w_proj, step)
                        at(w_reduce, step - 1)
                        at(w_sqrt, step - 1)
                        at(w_o, step - 2)
                        if not tq_off:
                            at(w_tq, step - 2)
                        at(w_xn, step - 1)
                        if not tq_off:
                            at(w_dest, step - 2)

                with ExitStack() as s1:
                    wkv = sb(s1, "wkv_s", [128, 8 * KV_COLS], BF16)
                    wkv_v = wkv[:].rearrange("p (k c) -> p k c", k=8)
                    wB = Buf("wkv")
                    wsrc = wkv_d[l].rearrange("p (k c) -> p k c", k=8)
                    for i in range(4):
                        T.dma("pool", sW[i], out=wkv_v[:, 2 * i:2 * i + 2, :], in_=wsrc[:, 2 * i:2 * i + 2, :],
                              writes=[wB], max_dma_last_dim=4096)
                    small_params(s1)
                    qsrc = wq_d[l].rearrange("p (k c) -> p k c", k=8)

                    def issue_wq(i):
                        T.dma("pool", sW[4 + i], out=wq_v[:, 2 * i:2 * i + 2, :], in_=qsrc[:, 2 * i:2 * i + 2, :],
                              writes=[wqB], max_dma_last_dim=4096)
                    WQ_STEPS = {5: 0, 7: 1, 9: 2, 11: 3}

                    def pre_fn(tb):
                        if tb in WQ_STEPS:
                            issue_wq(WQ_STEPS[tb])
                    if l == layers[0]:
                        for tb in range(NB):
                            T.dma("sp", sX[tb], out=Xv[:, tb, :], in_=xv_d[:, tb, :], writes=[XB[tb]])
                        junk1 = sb(s1, "junk1", [128, D], BF16)
                        junk1B = Buf("junk1")
                        for tb in range(2):
                            stat_of(tb, junk1[:], junk1B)

                        def pre_fn(tb):
                            if tb in WQ_STEPS:
                                issue_wq(WQ_STEPS[tb])
                            if tb + 2 < NB:
                                stat_of(tb + 2, junk1[:], junk1B)

                    def v_fn(tb, bset):
                        b1, b2 = bset[1][0], bset[2][0]
                        act(lambda e: e.activation(out=vaugv[:, tb, :, 0:64],
                                                   in_=PS[b1][:, 256:384].rearrange("p (g d) -> p g d", g=2),
                                                   func=AF.Copy), reads=[PB[b1]], writes=[VB])
                        act(lambda e: e.activation(out=vbv[:, tb, :], in_=PS[b2][:, :], func=AF.Copy),
                            reads=[PB[b2]], writes=[VB])

                    def dest_k(tp, nch, tb, tpb):
                        act(lambda e: e.activation(out=KTv[:, :, tb * 128:(tb + 1) * 128], in_=tp, func=AF.Copy),
                            reads=[PB[tpb]], writes=[KTB])

                    run_pipeline(list(range(NB)), wkv_v, wB,
                                 [[(0, 0, 512), (1, 512, 384), (2, 896, 512)], [(3, 0, 512), (4, 512, 384), (5, 896, 512)]],
                                 [7, 7], [6, 6], 768, [(0, 8, 3), (8, 4, 1)], dest_k, v_fn, tq_off=1, pre_fn=pre_fn, ahead=True)
                    T.barrier()

                with ExitStack() as s2:
                    wo = sb(s2, "wo_s", [128, 8 * D], BF16)
                    wo_v = wo[:].rearrange("p (k c) -> p k c", k=8)
                    wB = wqB
                    woB = Buf("wo")
                    osrc = wo_d[l].rearrange("p (k c) -> p k c", k=8)
                    QT = sb(s2, "QT", [128, 8 * 512], BF16)
                    QTv = QT[:].rearrange("p (k t) -> p k t", k=8)
                    QTB = Buf("QT")
                    yT = sb(s2, "yT", [128, 8 * 512], BF16)
                    yTv = yT[:].rearrange("p (k t) -> p k t", k=8)
                    yTB = Buf("yT")
                    yav = a2bf[:, 0:2048].rearrange("p (n c) -> p n c", n=4)
                    yaB = Buf("ya")
                    Ew = [a2bf[:, 2048 + i * 512:2048 + (i + 1) * 512] for i in range(6)]
                    EwB = [Buf(f"Ew{i}") for i in range(6)]
                    Ed = [a2bf[:, 2048 + i * 1024:2048 + (i + 1) * 1024] for i in range(2)]
                    EdB = [Buf(f"Ed{i}") for i in range(2)]
                    sqd = a2bf[:, 4096:4608]
                    sqdB = Buf("sqd")
                    wst = sb(s2, "wst", [128, 32], F32)
                    wstB = [Buf("wst0"), Buf("wst1")]
                    Y0s, Y1s = scrA[:, 0:512], scrA[:, 512:1024]
                    Rcat = scrA[0:64, 1024:1536]
                    sdd, rr = scrA[:, 1536:2048], scrA[:, 2048:2560]
                    Y0B, Y1B, RcB, sddB, rrB = Buf("Y0s"), Buf("Y1s"), Buf("Rcat"), Buf("sdd"), Buf("rr")
                    sctr = [0]

                    for qt in range(4):
                        def dest_q(tp, nch, tb, tpb):
                            tl = tb % 4
                            act(lambda e: e.activation(out=QTv[:, :, tl * 128:(tl + 1) * 128], in_=tp, func=AF.Copy),
                                reads=[PB[tpb]], writes=[QTB])
                        run_pipeline([qt * 4 + tl for tl in range(4)], wq_v, wB,
                                     [[(0, 0, 512), (1, 512, 512)], [(2, 0, 512), (3, 512, 512)]],
                                     [7, 5], [6, 4], 1024, [(0, 8, 0), (8, 8, 2)], dest_q, None, tq_off=1)
                        T.barrier()
                        items = [(nl, g) for nl in range(4) for g in range(2)]

                        def win_A(idx):
                            nl, g = items[idx]
                            n = qt * 4 + nl
                            cs = [c for c in (n - 1, n, n + 1) if 0 <= c < NB]
                            for ci, c in enumerate(cs):
                                pair = sctr[0] % 2
                                sctr[0] += 1
                                b0 = 2 * pair
                                E = Ew[(idx % 2) * 3 + ci]
                                EB = EwB[(idx % 2) * 3 + ci]

                                def emit_s(pe, c=c, b0=b0, g=g, nl=nl):
                                    k0 = 4 if g == 0 else 5
                                    k1 = 4 if g == 1 else 5
                                    pe.matmul(PS[b0][:, 0:256].rearrange("p (a q) -> p a q", a=2),
                                              lhsT=KTv[0:64, k0, c * 128:(c + 1) * 128],
                                              rhs=QTv[0:64, 2 * g:2 * g + 2, nl * 128:(nl + 1) * 128],
                                              start=True, stop=True)
                                    return pe.matmul(PS[b0 + 1][:, 0:256].rearrange("p (a q) -> p a q", a=2),
                                                     lhsT=KTv[64:128, k1, c * 128:(c + 1) * 128],
                                                     rhs=QTv[64:128, 2 * g:2 * g + 2, nl * 128:(nl + 1) * 128],
                                                     start=True, stop=True)
                                pe_group(emit_s, reads=[KTB, QTB], writes=[PB[b0], PB[b0 + 1]])
                                act(lambda e, E=E, b0=b0: e.activation(
                                    out=E.rearrange("p (a q) -> p a q", a=2),
                                    in_=PSbig[:, b0 * 512:(b0 + 2) * 512].rearrange("p (a q) -> p a q", a=2)[:, :, 0:256],
                                    func=AF.Exp, scale=0.125), reads=[PB[b0], PB[b0 + 1]], writes=[EB])
                                if c != n:
                                    m = mge if c == n - 1 else mle
                                    dve(lambda e, E=E, m=m: e.tensor_tensor(
                                        out=E.rearrange("p (a q) -> p a q", a=4),
                                        in0=E.rearrange("p (a q) -> p a q", a=4),
                                        in1=m[:].unsqueeze(1).to_broadcast([128, 4, 128]), op=ALU.mult),
                                        reads=[EB, constB], writes=[EB])

                        def win_B(idx):
                            nl, g = items[idx]
                            n = qt * 4 + nl
                            cs = [c for c in (n - 1, n, n + 1) if 0 <= c < NB]
                            sp = idx % 2
                            ob = 4 + sp
                            Ov = PS[ob][:, 0:260].rearrange("p (h d) -> p h d", h=4)
                            Es = [Ew[sp * 3 + ci] for ci in range(len(cs))]

                            def emit_pv(pe):
                                ins = None
                                for hd in range(4):
                                    half, ci2 = hd % 2, hd // 2
                                    off = half * 256 + ci2 * 128
                                    for ci, c in enumerate(cs):
                                        ins = pe.matmul(Ov[:, hd, :], lhsT=Es[ci][:, off:off + 128],
                                                        rhs=vaugv[:, c, g, 0:65], start=(ci == 0),
                                                        stop=(ci == len(cs) - 1))
                                return ins
                            pe_group(emit_pv, reads=[EwB[sp * 3 + ci] for ci in range(len(cs))] + [VB], writes=[PB[ob]])
                            zz = wst[:, sp * 8:sp * 8 + 4]
                            rz = wst[:, 16 + sp * 8:16 + sp * 8 + 4]
                            dve(lambda e: e.tensor_tensor(out=zz, in0=Ov[:, :, 64], in1=esink[:, 4 * g:4 * g + 4], op=ALU.add),
                                reads=[PB[ob], miscB], writes=[wstB[sp]])
                            dve(lambda e: e.reciprocal(out=rz, in_=zz), reads=[wstB[sp]], writes=[wstB[sp]])
                            dve(lambda e: e.tensor_tensor(
                                out=yav[:, nl, g * 256:(g + 1) * 256].rearrange("p (h d) -> p h d", h=4),
                                in0=Ov[:, :, 0:64], in1=rz.unsqueeze(2).to_broadcast([128, 4, 64]),
                                op=ALU.mult), reads=[PB[ob], wstB[sp]], writes=[yaB])
                            if g == 1:
                                tpb = 6 + (nl % 2)
                                tp = ps_bf(tpb, 4)

                                def emit_ty(pe):
                                    ins = None
                                    for k in range(4):
                                        ins = pe.transpose(out=tp[:, k, :], in_=yav[:, nl, k * 128:(k + 1) * 128],
                                                           identity=ident[:])
                                    return ins
                                pe_group(emit_ty, reads=[yaB, constB], writes=[PB[tpb]])
                                act(lambda e: e.activation(out=yTv[:, 0:4, nl * 128:(nl + 1) * 128], in_=tp,
                                                           func=AF.Copy), reads=[PB[tpb]], writes=[yTB])

                        win_A(0)
                        for idx in range(len(items)):
                            if idx + 1 < len(items):
                                win_A(idx + 1)
                            win_B(idx)
                        T.barrier()
                        if qt == 0:
                            for i in range(4):
                                T.dma("pool", sW[i], out=wo_v[:, 2 * i:2 * i + 2, :], in_=osrc[:, 2 * i:2 * i + 2, :],
                                      writes=[woB], max_dma_last_dim=4096)
                        chunks = [(h, c) for h in range(4) for c in range(NB)]

                        def s_mm(i):
                            h, c = chunks[i]
                            b0 = 2 * (i % 2)

                            def emit(pe):
                                pe.matmul(PS[b0][:, :], lhsT=KTv[0:64, h, c * 128:(c + 1) * 128],
                                          rhs=QTv[0:64, 4 + h, :], start=True, stop=True)
                                return pe.matmul(PS[b0 + 1][:, :], lhsT=KTv[64:128, h, c * 128:(c + 1) * 128],
                                                 rhs=QTv[64:128, 4 + h, :], start=True, stop=True)
                            pe_group(emit, reads=[KTB, QTB], writes=[PB[b0], PB[b0 + 1]])

                        def part1(h):
                            y1b = 6 + (h % 2)
                            act(lambda e: e.activation(out=Y0s, in_=PS[4][:, :], func=AF.Copy), reads=[PB[4]], writes=[Y0B])
                            dve(lambda e: e.tensor_copy(out=Rcat, in_=PS[5][0:64, :]), reads=[PB[5]], writes=[RcB])
                            dve(lambda e: e.tensor_copy(out=Y1s, in_=PS[y1b][:, :]), reads=[PB[y1b]], writes=[Y1B])
                            dve(lambda e: e.reciprocal(out=Rcat, in_=Rcat), reads=[RcB], writes=[RcB])

                        def make_stages(h):
                            sbk = 6 + (h % 2)

                            def stA():
                                pe_group(lambda pe: pe.matmul(PS[sbk][:, :], lhsT=sel[:, 0:128], rhs=Rcat, start=True, stop=True),
                                         reads=[RcB, constB], writes=[PB[sbk]])
                                dve(lambda e: e.tensor_tensor(out=Y0s, in0=PS[sbk][:, :], in1=Y0s, op=ALU.mult),
                                    reads=[PB[sbk], Y0B], writes=[Y0B])

                            def stB():
                                pe_group(lambda pe: pe.matmul(PS[sbk][:, :], lhsT=sel[:, 128:256], rhs=Rcat, start=True, stop=True),
                                         reads=[RcB, constB], writes=[PB[sbk]])
                                dve(lambda e: e.tensor_tensor(out=Y1s, in0=PS[sbk][:, :], in1=Y1s, op=ALU.mult),
                                    reads=[PB[sbk], Y1B], writes=[Y1B])
                                dve(lambda e: e.scalar_tensor_tensor(out=Y0s, in0=Y1s, scalar=neglam, in1=Y0s, op0=ALU.mult,
                                                                     op1=ALU.add), reads=[Y0B, Y1B, miscB], writes=[Y0B])
                                dve(lambda e: e.tensor_tensor(out=sqd, in0=Y0s, in1=Y0s, op=ALU.mult), reads=[Y0B], writes=[sqdB])

                            def stC():
                                pe_group(lambda pe: pe.matmul(PS[sbk][:, :], lhsT=ones[:], rhs=sqd, start=True, stop=True),
                                         reads=[sqdB, constB], writes=[PB[sbk]])
                                act(lambda e: e.activation(out=sdd, in_=PS[sbk][:, :], func=AF.Ln, bias=eps_ap,
                                                           scale=1.0 / 128), reads=[PB[sbk], miscB], writes=[sddB])

                            def stD():
                                act(lambda e: e.activation(out=rr, in_=sdd, func=AF.Exp, scale=-0.5),
                                    reads=[sddB], writes=[rrB])
                                dve(lambda e: e.scalar_tensor_tensor(out=yTv[:, 4 + h, :], in0=Y0s, scalar=gsub, in1=rr,
                                                                     op0=ALU.mult, op1=ALU.mult),
                                    reads=[Y0B, rrB, miscB], writes=[yTB])
                            return [stA, stB, stC, stD]

                        pending = []
                        s_mm(0)
                        for i, (h, c) in enumerate(chunks):
                            par = i % 2
                            if i + 1 < len(chunks):
                                s_mm(i + 1)
                            act(lambda e, par=par: e.activation(out=Ed[par], in_=PSbig[:, 2 * par * 512:(2 * par + 2) * 512],
                                                                func=AF.Exp, scale=0.125),
                                reads=[PB[2 * par], PB[2 * par + 1]], writes=[EdB[par]])
                            if c == 0 and h > 0:
                                part1(h - 1)
                                pending = make_stages(h - 1)
                            y1b = 6 + (h % 2)

                            def emit_pv(pe, c=c, par=par, h=h, y1b=y1b):
                                st, sp_ = (c == 0), (c == NB - 1)
                                pe.matmul(PS[4][:, :], lhsT=vbv[:, c, h * 128:(h + 1) * 128], rhs=Ed[par][:, 0:512],
                                          start=st, stop=sp_)
                                pe.matmul(PS[y1b][:, :], lhsT=vbv[:, c, h * 128:(h + 1) * 128], rhs=Ed[par][:, 512:1024],
                                          start=st, stop=sp_)
                                pe.matmul(PS[5][0:32, :], lhsT=ones[:, 0:32], rhs=Ed[par][:, 0:512], start=st, stop=sp_)
                                return pe.matmul(PS[5][32:64, :], lhsT=ones[:, 0:32], rhs=Ed[par][:, 512:1024],
                                                 start=st, stop=sp_)
                            pe_group(emit_pv, reads=[EdB[par], VB, constB], writes=[PB[4], PB[5], PB[y1b]])
                            if pending and c in (4, 7, 10, 12):
                                pending.pop(0)()
                        part1(3)
                        for f in make_stages(3):
                            f()
                        T.barrier()
                        for tl in range(4):
                            tb = qt * 4 + tl
                            b0, b1 = 2 * (tl % 2), 2 * (tl % 2) + 1

                            def emit_o(pe, tl=tl, b0=b0, b1=b1):
                                ins = None
                                for k in range(8):
                                    pe.matmul(PS[b0][:, :], lhsT=yTv[:, k, tl * 128:(tl + 1) * 128],
                                              rhs=wo_v[:, k, 0:512], start=(k == 0), stop=(k == 7))
                                    ins = pe.matmul(PS[b1][:, :], lhsT=yTv[:, k, tl * 128:(tl + 1) * 128],
                                                    rhs=wo_v[:, k, 512:1024], start=(k == 0), stop=(k == 7))
                                return ins
                            pe_group(emit_o, reads=[yTB, woB], writes=[PB[b0], PB[b1]])
                            for hf, bk in ((0, b0), (1, b1)):
                                dve(lambda e, tb=tb, hf=hf, bk=bk: e.tensor_tensor(
                                    out=Xv[:, tb, hf * 512:(hf + 1) * 512], in0=PS[bk][:, :],
                                    in1=Xv[:, tb, hf * 512:(hf + 1) * 512], op=ALU.add),
                                    reads=[PB[bk], XB[tb]], writes=[XB[tb]])
                            stat_of(tb, hb[0][:], hbB[0])
                        T.barrier()

            with ExitStack() as sf:
                hTa = sb(sf, "hTa", [128, 8 * S], BF16)
                hTav = hTa[:].rearrange("p (k t) -> p k t", k=8)
                hTaB = Buf("hTa")
                uT = sb(sf, "uT", [128, 6 * S], BF16)
                uTv = uT[:].rearrange("p (j t) -> p j t", j=6)
                uTB = Buf("uT")
                wdn = [sb(sf, f"wdn{i}", [128, 6 * D], BF16) for i in range(2)]
                wdnv = [t[:].rearrange("p (j c) -> p j c", j=6) for t in wdn]
                wdnB = [Buf("wdn0"), Buf("wdn1")]
                wup = [sb(sf, f"wup{i}", [128, 8 * 256], BF16) for i in range(3)]
                wupv = [t[:].rearrange("p (k c) -> p k c", k=8) for t in wup]
                wupB = [Buf(f"wup{i}") for i in range(3)]
                graw = sb(sf, "graw", [128, S + 2], F32)
                grawB = Buf("graw")
                c1 = sb(sf, "c1", [128, S], F32)
                c1B = Buf("c1")
                val = sb(sf, "val", [128, S], F32)
                valB = Buf("val")
                hbf = sb(sf, "hbf", [128, D], BF16)
                hbfB = Buf("hbf")
                cw = sm(l, O_CW, 66).rearrange("p (j t) -> p j t", t=3)
                cb = sm(l, O_CB, 22)
                dsrc = wdn_d[l].rearrange("p (j c) -> p j c", j=NJ)

                def load_up(j):
                    s = j % 3
                    T.dma("pool", sU[s], out=wup[s][:], in_=wup_d[l, j], writes=[wupB[s]], max_dma_last_dim=4096)

                def load_dn(gi):
                    grp = FF_GROUPS[gi]
                    s = gi % 2
                    T.dma("pool", sD[s], out=wdnv[s][:, 0:len(grp), :], in_=dsrc[:, grp[0]:grp[0] + len(grp), :],
                          writes=[wdnB[s]], max_dma_last_dim=4096)

                load_up(0)
                load_up(1)
                load_dn(0)
                load_dn(1)
                pool(lambda e: e.memset(graw[:, 0:1], 0.0), writes=[grawB])
                pool(lambda e: e.memset(graw[:, S + 1:S + 2], 0.0), writes=[grawB])
                hbf2 = [hbf, sb(sf, "hbf2", [128, D], BF16)]
                hbf2B = [hbfB, Buf("hbf2")]

                def f_hb(tb):
                    pr = tb % 2
                    act(lambda e: e.activation(out=hbf2[pr][:], in_=Xv[:, tb, :], func=AF.Copy, scale=rstd[:, tb:tb + 1]),
                        reads=[XB[tb], rstdBs[tb]], writes=[hbf2B[pr]])

                def f_T(tb):
                    pr = tb % 2
                    bank = 6 + pr
                    tp = ps_bf(bank, 8)

                    def emit(pe):
                        ins = None
                        for k in range(8):
                            ins = pe.transpose(out=tp[:, k, :], in_=hbf2[pr][:, k * 128:(k + 1) * 128], identity=ident[:])
                        return ins
                    pe_group(emit, reads=[hbf2B[pr], constB], writes=[PB[bank]])
                    dve(lambda e: e.tensor_tensor(out=hTav[:, :, tb * 128:(tb + 1) * 128], in0=tp,
                                                  in1=gF.unsqueeze(2).to_broadcast([128, 8, 128]), op=ALU.mult),
                        reads=[PB[bank], constB], writes=[hTaB])
                f_hb(0)
                for tb in range(NB):
                    if tb + 1 < NB:
                        f_hb(tb + 1)
                    f_T(tb)

                slot = [0]

                def next_slot():
                    s_ = slot[0] % 4
                    slot[0] += 1
                    return 2 * s_, 2 * s_ + 1

                def up_jobs(j):
                    s = j % 3
                    for pr in range(2):
                        tts = (2 * pr, 2 * pr + 1)
                        for which in range(2):
                            b0, b1 = next_slot()
                            c0 = which * 128

                            def emit(pe, b0=b0, b1=b1, c0=c0, tts=tts):
                                ins = None
                                for k in range(8):
                                    pe.matmul(PS[b0][:, :], lhsT=wupv[s][:, k, c0:c0 + 128],
                                              rhs=hTav[:, k, tts[0] * 512:(tts[0] + 1) * 512], start=(k == 0), stop=(k == 7))
                                    ins = pe.matmul(PS[b1][:, :], lhsT=wupv[s][:, k, c0:c0 + 128],
                                                    rhs=hTav[:, k, tts[1] * 512:(tts[1] + 1) * 512], start=(k == 0),
                                                    stop=(k == 7))
                                return ins
                            pe_group(emit, reads=[wupB[s], hTaB], writes=[PB[b0], PB[b1]])
                            for bk, tt in ((b0, tts[0]), (b1, tts[1])):
                                if which == 0:
                                    act(lambda e, tt=tt, bk=bk: e.activation(out=graw[:, 1 + tt * 512:1 + (tt + 1) * 512],
                                                                             in_=PS[bk][:, :], func=AF.Copy),
                                        reads=[PB[bk]], writes=[grawB])
                                    act(lambda e, tt=tt, bk=bk: e.activation(out=c1[:, tt * 512:(tt + 1) * 512],
                                                                             in_=PS[bk][:, :], func=AF.Identity,
                                                                             scale=cw[:, j, 1:2], bias=cb[:, j:j + 1]),
                                        reads=[PB[bk], constB], writes=[c1B])
                                else:
                                    act(lambda e, tt=tt, bk=bk: e.activation(out=val[:, tt * 512:(tt + 1) * 512],
                                                                             in_=PS[bk][:, :], func=AF.Copy),
                                        reads=[PB[bk]], writes=[valB])

                def chain_a(j):
                    dve(lambda e: e.scalar_tensor_tensor(out=c1[:], in0=graw[:, 0:S], scalar=cw[:, j, 0:1],
                                                         in1=c1[:], op0=ALU.mult, op1=ALU.add),
                        reads=[grawB, c1B, constB], writes=[c1B])
                    dve(lambda e: e.scalar_tensor_tensor(out=c1[:], in0=graw[:, 2:S + 2], scalar=cw[:, j, 2:3],
                                                         in1=c1[:], op0=ALU.mult, op1=ALU.add),
                        reads=[grawB, c1B, constB], writes=[c1B])
                    act(lambda e: e.activation(out=c1[:], in_=c1[:], func=AF.Silu), reads=[c1B], writes=[c1B])

                def chain_b(jl):
                    dve(lambda e: e.tensor_tensor(out=uTv[:, jl, :], in0=c1[:], in1=val[:], op=ALU.mult),
                        reads=[c1B, valB], writes=[uTB])

                def down(gi):
                    grp = FF_GROUPS[gi]
                    s = gi % 2
                    for tb in range(NB):
                        b0, b1 = next_slot()

                        def emit_d(pe, tb=tb, b0=b0, b1=b1):
                            ins = None
                            for jl in range(len(grp)):
                                pe.matmul(PS[b0][:, :], lhsT=uTv[:, jl, tb * 128:(tb + 1) * 128],
                                          rhs=wdnv[s][:, jl, 0:512], start=(jl == 0), stop=(jl == len(grp) - 1))
                                ins = pe.matmul(PS[b1][:, :], lhsT=uTv[:, jl, tb * 128:(tb + 1) * 128],
                                                rhs=wdnv[s][:, jl, 512:1024], start=(jl == 0), stop=(jl == len(grp) - 1))
                            return ins
                        pe_group(emit_d, reads=[uTB, wdnB[s]], writes=[PB[b0], PB[b1]])
                        for hf, bk in ((0, b0), (1, b1)):
                            dve(lambda e, tb=tb, hf=hf, bk=bk: e.tensor_tensor(
                                out=Xv[:, tb, hf * 512:(hf + 1) * 512], in0=PS[bk][:, :],
                                in1=Xv[:, tb, hf * 512:(hf + 1) * 512], op=ALU.add),
                                reads=[PB[bk], XB[tb]], writes=[XB[tb]])
                        if gi == len(FF_GROUPS) - 1 and l != layers[-1]:
                            stat_of(tb, hbf[:], hbfB)
                        if gi == len(FF_GROUPS) - 1 and l == layers[-1]:
                            T.dma("sp", sX[tb], out=yv_d[:, tb, :], in_=Xv[:, tb, :], reads=[XB[tb]])

                prev = None
                for gi, grp in enumerate(FF_GROUPS):
                    for jl, j in enumerate(grp):
                        if j + 2 < NJ:
                            load_up(j + 2)
                        up_jobs(j)
                        chain_a(j)
                        if jl == 0 and prev is not None:
                            down(prev)
                            if gi + 1 < len(FF_GROUPS):
                                load_dn(gi + 1)
                        chain_b(jl)
                    prev = gi
                down(prev)
                T.barrier()

        T.barrier()
    return nc


def _prep_shared(inp):
    f = lambda a: np.ascontiguousarray(np.asarray(a, dtype=np.float32))
    w_in = f(inp["w_in"])
    L = w_in.shape[0]
    qa, ka, va = w_in[:, :, 0:512], w_in[:, :, 512:640], w_in[:, :, 640:768]
    qb, kb, vb = w_in[:, :, 768:1280], w_in[:, :, 1280:1792], w_in[:, :, 1792:2304]
    kaS = np.concatenate([ka[:, :, 64:128], ka[:, :, 0:64]], axis=2)
    wkv = np.concatenate([kb, ka, kaS, va, vb], axis=2)
    wq = np.concatenate([qa, qb], axis=2)

    def pk(w):
        Lc, K, C = w.shape
        return np.ascontiguousarray(w.reshape(Lc, K // 128, 128, C).transpose(0, 2, 1, 3).reshape(Lc, 128, -1))
    w_up = f(inp["w_up"])
    gate, val = w_up[:, :, :DFF], w_up[:, :, DFF:]
    gv = np.concatenate([gate.reshape(L, D, NJ, 1, 128), val.reshape(L, D, NJ, 1, 128)], axis=3)
    wup = np.ascontiguousarray(gv.reshape(L, 8, 128, NJ, 256).transpose(0, 3, 2, 1, 4).reshape(L, NJ, 128, 8 * 256))
    small = np.zeros((128, L, NS), np.float32)
    for l in range(L):
        small[:, l, O_GA:O_GA + 8] = f(inp["g_attn"])[l].reshape(8, 128).T
        small[:, l, O_GF:O_GF + 8] = f(inp["g_ffn"])[l].reshape(8, 128).T
        small[:, l, O_QKG:O_QKG + 256] = np.concatenate([f(inp["qn_a"])[l], f(inp["kn_a"])[l], f(inp["qn_b"])[l],
                                                         f(inp["kn_b"])[l]])[None, :]
        small[:, l, O_SINK:O_SINK + 8] = f(inp["sink"])[l][None, :]
        small[:, l, O_LAM:O_LAM + 256] = np.concatenate([f(inp["lq1"])[l], f(inp["lk1"])[l], f(inp["lq2"])[l],
                                                         f(inp["lk2"])[l]])[None, :]
        small[:, l, O_SUB] = f(inp["subln"])[l]
        small[:, l, O_CW:O_CW + 66] = f(inp["conv_w"])[l].reshape(3, NJ, 128).transpose(2, 1, 0).reshape(128, 66)
        small[:, l, O_CB:O_CB + 22] = f(inp["conv_b"])[l].reshape(NJ, 128).T
    inv = (1.0 / (np.float32(10000.0) ** (np.arange(0, 64, 2, dtype=np.float32) / np.float32(64)))).astype(np.float32)
    ang = (np.arange(S, dtype=np.float32)[:, None] * inv[None, :]).astype(np.float32)
    cos = np.cos(ang).astype(np.float32).reshape(NB, 128, 32).transpose(1, 0, 2)
    sin = np.sin(ang).astype(np.float32).reshape(NB, 128, 32).transpose(1, 0, 2)
    sinS = np.stack([sin, -sin], axis=2)
    return {
        "cosT": np.ascontiguousarray(cos.reshape(128, NB * 32)),
        "sinS": np.ascontiguousarray(sinS.reshape(128, NB * 64)),
        "small": np.ascontiguousarray(small.reshape(128, L * NS)),
        "wkv": pk(wkv), "wq": pk(wq), "wo": pk(f(inp["w_out"])), "wup": wup,
        "wdn": np.ascontiguousarray(f(inp["w_down"]).reshape(L, NJ, 128, D).transpose(0, 2, 1, 3).reshape(L, 128, NJ * D)),
    }


_NC_CACHE = {}


def _get_nc(layers):
    key = tuple(layers)
    if key not in _NC_CACHE:
        _NC_CACHE[key] = build(layers)
    return _NC_CACHE[key]


def kernel(**inputs):
    x = np.ascontiguousarray(np.asarray(inputs["x"], dtype=np.float32))
    shared = _prep_shared(inputs)
    nc = _get_nc((0, 1))
    in_maps = [dict(shared, x=x[b]) for b in range(8)]
    res = run_bass_kernel_spmd(nc, in_maps, core_ids=list(range(8)))
    return np.stack([np.asarray(r["y"], dtype=np.float32) for r in res.results], axis=0)
```
